# Optimizing a Trainium2 kernel written in Bass

```python
import jax
import jax.numpy as jnp
from jax import lax
import numpy as np


D_MODEL = 1024
BATCH = 16
SEQ = 4096
DEPTH = 4

CHUNK = 64
N_META = 16
N_HEADS_A = 8
HEAD_DIM = 64
D_A = N_HEADS_A * HEAD_DIM
N_IDX_HEADS = 8
IDX_DIM = HEAD_DIM
TOPK_MAX = 256
POOL_WINDOWS = (2, 4, 8, 16)
N_POOL_GROUPS = len(POOL_WINDOWS)
D_B = D_MODEL // 2
POOL_GROUP_DIM = D_B // N_POOL_GROUPS
Q_BLOCK = 128
ROPE_THETA = 10000.0
EPS = 1e-6
NEG_INF = -1e30
IN_WIDTHS = (D_A, HEAD_DIM, HEAD_DIM, D_A, D_B, D_B, N_IDX_HEADS * IDX_DIM, IDX_DIM, N_IDX_HEADS, 2 * D_MODEL)
D_IN = sum(IN_WIDTHS)

kernel_name = 'hybrid_dsa_pool_gated_block'


def rms_norm(x, g):
    xf = x.astype(jnp.float32)
    y = xf * lax.rsqrt(jnp.mean(xf * xf, axis=-1, keepdims=True) + EPS)
    return (y * g.astype(jnp.float32)).astype(x.dtype)


def rope_tables(length, dim):
    inv_freq = 1.0 / (ROPE_THETA ** (jnp.arange(0, dim, 2, dtype=jnp.float32) / dim))
    ang = jnp.arange(length, dtype=jnp.float32)[:, None] * inv_freq[None, :]
    ang = jnp.concatenate([ang, ang], axis=-1)
    return jnp.cos(ang), jnp.sin(ang)


def apply_rope(x, cos, sin):
    half = x.shape[-1] // 2
    xf = x.astype(jnp.float32)
    rot = jnp.concatenate([-xf[..., half:], xf[..., :half]], axis=-1)
    return (xf * cos + rot * sin).astype(x.dtype)


def chunk_id(pos):
    return jnp.where(pos < N_META, 0, 1 + (pos - N_META) // CHUNK)


def indexer_sparse_attention(q, k, v, q_idx, k_idx, w_idx, k_top):
    B, T = q.shape[0], q.shape[1]
    n_blk = -(-T // Q_BLOCK)
    t_pad = n_blk * Q_BLOCK

    def to_blocks(a):
        a = jnp.pad(a, [(0, 0), (0, t_pad - T)] + [(0, 0)] * (a.ndim - 2))
        return jnp.moveaxis(a.reshape((B, n_blk, Q_BLOCK) + a.shape[2:]), 1, 0)

    key_chunk = chunk_id(jnp.arange(T))
    query_chunk = chunk_id(jnp.arange(t_pad)).reshape(n_blk, Q_BLOCK)
    k_idx_f = k_idx.astype(jnp.float32)
    idx_scale = (N_IDX_HEADS ** -0.5) * (IDX_DIM ** -0.5)
    attn_scale = HEAD_DIM ** -0.5
    gather = jax.vmap(lambda table, ids: table[ids])

    def one_block(args):
        qb, qib, wib, qcb = args
        s = jnp.einsum('bqhd,bkd->bqhk', qib.astype(jnp.float32), k_idx_f)
        score = jnp.einsum('bqhk,bqh->bqk', jax.nn.relu(s), wib.astype(jnp.float32) * idx_scale)
        visible = key_chunk[None, :] <= qcb[:, None]
        score = jnp.where(visible[None], score, NEG_INF)
        _, sel = lax.top_k(score, k_top)
        valid = key_chunk[sel] <= qcb[None, :, None]
        kg = gather(k, sel)
        vg = gather(v, sel)
        logits = jnp.einsum('bqhd,bqkd->bqhk', qb, kg).astype(jnp.float32) * attn_scale
        logits = jnp.where(valid[:, :, None, :], logits, NEG_INF)
        p = jax.nn.softmax(logits, axis=-1).astype(vg.dtype)
        return jnp.einsum('bqhk,bqkd->bqhd', p, vg)

    out = lax.map(one_block, (to_blocks(q), to_blocks(q_idx), to_blocks(w_idx), query_chunk))
    return jnp.moveaxis(out, 0, 1).reshape(B, t_pad, N_HEADS_A, HEAD_DIM)[:, :T]


def multiscale_pool(u, pool_w, pool_b, pool_s):
    B, T, C = u.shape
    G = POOL_GROUP_DIM
    uf = u.astype(jnp.float32)
    cs = jnp.concatenate([jnp.zeros((B, 1, C), jnp.float32), jnp.cumsum(uf, axis=1)], axis=1)
    t1 = jnp.arange(1, T + 1, dtype=jnp.float32)
    means = []
    for g, w in enumerate(POOL_WINDOWS):
        csg = cs[..., g * G:(g + 1) * G]
        lag = jnp.concatenate([jnp.zeros((B, w, G), jnp.float32), csg[:, :T + 1 - w]], axis=1)
        cnt = jnp.minimum(t1, float(w))
        means.append((csg[:, 1:] - lag[:, 1:]) / cnt[None, :, None])
    pooled = (jnp.concatenate(means, axis=-1) - uf).astype(u.dtype).reshape(B, T, N_POOL_GROUPS, G)
    mixed = jnp.einsum('btgc,gcd->btgd', pooled, pool_w) + pool_b.reshape(N_POOL_GROUPS, G)
    return mixed.reshape(B, T, C) * pool_s


def hybrid_layer(x, norm_g, w_in, qn_g, kn_g, pool_w, pool_b, pool_s, w_a, w_b, w_out, cos, sin, k_top):
    B, T, _ = x.shape
    h = rms_norm(x, norm_g)
    proj = jnp.einsum('btd,de->bte', h, w_in)
    q, k, v, gate_a, u_b, gate_b, q_idx, k_idx, w_idx, merge = jnp.split(
        proj, np.cumsum(IN_WIDTHS)[:-1].tolist(), axis=-1)
    cos_h, sin_h = cos[:, None, :], sin[:, None, :]
    q = apply_rope(rms_norm(q.reshape(B, T, N_HEADS_A, HEAD_DIM), qn_g), cos_h, sin_h)
    k = apply_rope(rms_norm(k, kn_g), cos, sin)
    q_idx = apply_rope(q_idx.reshape(B, T, N_IDX_HEADS, IDX_DIM), cos_h, sin_h)
    k_idx = apply_rope(k_idx, cos, sin)
    attn = indexer_sparse_attention(q, k, v, q_idx, k_idx, w_idx, k_top).reshape(B, T, D_A)
    y_a = jnp.einsum('bte,ed->btd', attn * jax.nn.silu(gate_a), w_a)
    pooled = multiscale_pool(u_b, pool_w, pool_b, pool_s)
    y_b = jnp.einsum('bte,ed->btd', pooled * jax.nn.silu(gate_b), w_b)
    g_a, g_b = jnp.split(jax.nn.sigmoid(merge), 2, axis=-1)
    return x + jnp.einsum('btd,de->bte', g_a * y_a + g_b * y_b, w_out)


def setup_inputs(seed: int = 0) -> dict:
    key = jax.random.key(seed)
    ks = jax.random.split(key, 12)

    def nrm(k, shape, scale):
        return scale * jax.random.normal(k, shape, jnp.float32)

    return {
        'x': nrm(ks[0], (BATCH, SEQ, D_MODEL), 1.0),
        'meta_tokens': nrm(ks[1], (N_META, D_MODEL), 1.0),
        'norm_gain': 1.0 + nrm(ks[2], (DEPTH, D_MODEL), 0.05),
        'w_in': nrm(ks[3], (DEPTH, D_MODEL, D_IN), D_MODEL ** -0.5),
        'q_norm_gain': 1.0 + nrm(ks[4], (DEPTH, HEAD_DIM), 0.05),
        'k_norm_gain': 1.0 + nrm(ks[5], (DEPTH, HEAD_DIM), 0.05),
        'pool_w': nrm(ks[6], (DEPTH, N_POOL_GROUPS, POOL_GROUP_DIM, POOL_GROUP_DIM), POOL_GROUP_DIM ** -0.5),
        'pool_b': nrm(ks[7], (DEPTH, D_B), 0.02),
        'pool_scale': 1.0 + nrm(ks[8], (DEPTH, D_B), 0.1),
        'w_branch_a': nrm(ks[9], (DEPTH, D_A, D_MODEL), D_A ** -0.5),
        'w_branch_b': nrm(ks[10], (DEPTH, D_B, D_MODEL), D_B ** -0.5),
        'w_out': nrm(ks[11], (DEPTH, D_MODEL, D_MODEL), D_MODEL ** -0.5),
    }


def reference(x, meta_tokens, norm_gain, w_in, q_norm_gain, k_norm_gain, pool_w, pool_b, pool_scale,
              w_branch_a, w_branch_b, w_out):
    B, S, _ = x.shape
    k_top = min(TOPK_MAX, S // 4)
    meta = jnp.broadcast_to(meta_tokens.astype(x.dtype)[None], (B, N_META, D_MODEL))
    h = jnp.concatenate([meta, x], axis=1)
    T = S + N_META
    cos, sin = rope_tables(T, HEAD_DIM)
    for l in range(DEPTH):
        h = hybrid_layer(h, norm_gain[l], w_in[l], q_norm_gain[l], k_norm_gain[l], pool_w[l], pool_b[l],
                         pool_scale[l], w_branch_a[l], w_branch_b[l], w_out[l], cos, sin, k_top)
    return h[:, N_META:]
```

```python
import numpy as np
from contextlib import ExitStack
import concourse.bass as bass
import concourse.mybir as mybir
from concourse.bass_utils import run_bass_kernel_spmd

F32 = mybir.dt.float32
BF16 = mybir.dt.bfloat16
AF = mybir.ActivationFunctionType
ALU = mybir.AluOpType
AX = mybir.AxisListType

D = 1024
D_IN = 4808
C_Q, C_K, C_V, C_GA, C_U, C_GB, C_QI, C_KI, C_WI, C_M = 0, 512, 576, 640, 1152, 1664, 2176, 2688, 2752, 2760
EPS = 1e-6
BIG = 1.0e30
POOL_W = (2, 4, 8, 16)


class Buf:
    __slots__ = ("name", "w", "r")

    def __init__(self, name=""):
        self.name = name
        self.w = {}
        self.r = {}


class _Rec:
    def __init__(self):
        self.call = None

    def __getattr__(self, name):
        def m(*a, **k):
            self.call = (name, a, k)
            return self
        return m


class Sched:
    ENGS = ("pe", "act", "dve", "pool", "sp")

    def __init__(self, nc, stack, self_sync=True):
        self.nc = nc
        self.stack = stack
        self.streams = {e: [] for e in self.ENGS}
        self.seen = {e: {} for e in self.ENGS}
        self.sems = {}
        self.cnt = {}
        self.self_sync = self_sync
        self.label = ""
        self._cap = None
        self.labels = {e: [] for e in self.ENGS}
        for e in ("pe", "act", "dve", "pool"):
            self.sems[e] = stack.enter_context(nc.semaphore("s_" + e))
            self.cnt[e] = 0

    def _chan(self, name):
        if name not in self.sems:
            self.sems[name] = self.stack.enter_context(self.nc.semaphore("c_" + name))
            self.cnt[name] = 0

    @staticmethod
    def _deps(reads, writes):
        deps = {}
        for b in reads:
            for k, v in b.w.items():
                if deps.get(k, 0) < v:
                    deps[k] = v
            if b.name.startswith("ps"):
                for k, v in b.r.items():
                    if deps.get(k, 0) < v:
                        deps[k] = v
        for b in writes:
            for k, v in b.w.items():
                if deps.get(k, 0) < v:
                    deps[k] = v
            for k, v in b.r.items():
                if deps.get(k, 0) < v:
                    deps[k] = v
        return deps

    def _waits(self, eng, deps, noself=False):
        waits = []
        seen = self.seen[eng]
        for k, v in deps.items():
            if k == eng and (eng == "pe" or noself or not self.self_sync):
                continue
            if seen.get(k, 0) >= v:
                continue
            seen[k] = v
            waits.append((k, v))
        return waits

    def capture(self, thunk):
        self._cap = []
        try:
            thunk()
        finally:
            lst, self._cap = self._cap, None
        return lst

    def replay(self, *lists):
        lists = [l for l in lists if l]
        pos = [0] * len(lists)
        tot = sum(len(l) for l in lists)
        for _ in range(tot):
            best, bi = None, -1
            for i, l in enumerate(lists):
                if pos[i] < len(l):
                    f = pos[i] / len(l)
                    if best is None or f < best:
                        best, bi = f, i
            kind, a, k, lab = lists[bi][pos[bi]]
            pos[bi] += 1
            self.label = lab
            if kind == "op":
                self.op(*a, **k)
            else:
                self.dma(*a, **k)

    def op(self, eng, fn, reads=(), writes=(), partial=False, noself=False):
        if getattr(self, "_cap", None) is not None:
            r = _Rec()
            fn(r)
            call = r.call
            self._cap.append(("op", (eng, (lambda e, call=call: getattr(e, call[0])(*call[1], **call[2]))),
                              dict(reads=list(reads), writes=list(writes), partial=partial, noself=noself), self.label))
            return
        waits = self._waits(eng, self._deps(reads, writes), noself)
        self.cnt[eng] += 1
        s = self.cnt[eng]
        r = _Rec()
        fn(r)
        self.streams[eng].append((waits, r.call, eng, 1))
        self.labels[eng].append((self.label, 2 if r.call[2].get("accum_out") is not None else 1))
        for b in writes:
            if not partial:
                b.w = {}
                b.r = {}
            b.w[eng] = s
        for b in reads:
            b.r[eng] = s

    def dma(self, ch, out_ap, in_ap, reads=(), writes=(), q="sp", **kw):
        if getattr(self, "_cap", None) is not None:
            self._cap.append(("dma", (ch, out_ap, in_ap), dict(reads=list(reads), writes=list(writes), q=q, **kw), self.label))
            return
        self._chan(ch)
        deps = self._deps(reads, writes)
        if self.cnt[ch] > 0:
            deps[ch] = max(deps.get(ch, 0), self.cnt[ch])
        waits = self._waits(q, deps)
        self.cnt[ch] += 16
        s = self.cnt[ch]
        self.streams[q].append((waits, ("dma_start", (), dict(out=out_ap, in_=in_ap, **kw)), ch, 16))
        for b in writes:
            b.w = {ch: s}
            b.r = {}
        for b in reads:
            b.r[ch] = s

    def finish(self):
        waits = []
        for k, v in self.cnt.items():
            if k in ("pe", "act", "dve", "pool"):
                continue
            if v > 0:
                waits.append((k, v))
        self.streams["sp"].append((waits, None, None, 0))

    def emit(self):
        nc = self.nc
        sems = self.sems

        def runner(stream):
            def body(e):
                for waits, fn, key, inc in stream:
                    for k, v in waits:
                        e.wait_ge(sems[k], v)
                    if fn is not None:
                        name, a, k = fn
                        getattr(e, name)(*a, **k).then_inc(sems[key], inc)
            return body

        with nc.Block() as block:
            block.tensor(runner(self.streams["pe"]))
            block.scalar(runner(self.streams["act"]))
            block.vector(runner(self.streams["dve"]))
            block.gpsimd(runner(self.streams["pool"]))
            block.sync(runner(self.streams["sp"]))


def bc_mid(ap2d, reps):
    p, f = ap2d.shape
    return ap2d.unsqueeze(1).to_broadcast([p, reps, f])


def bc_last(ap2d, reps):
    p, g = ap2d.shape
    return ap2d.unsqueeze(2).to_broadcast([p, g, reps])


def build_program(NS, NT, NL, KTOP, ITER=22, self_sync=True, dbg=False):
    T = 16 + 128 * NT
    NTT = NT + 1
    SCW = max(T, 4096)
    LASTL = NL - 1 if not dbg else 10 ** 6
    nc = bass.Bass("TRN2", target_bir_lowering=False)
    stack = ExitStack()

    def dram(name, shape, dt=F32, kind="ExternalInput"):
        return nc.dram_tensor(name, list(shape), dt, kind=kind).ap()

    x_d = dram("x", [NS, 128 * NT, D])
    meta_d = dram("meta", [16, D])
    ng_d = dram("norm_gain", [NL, D])
    win_d = dram("w_in", [NL, D, D_IN])
    qg_d = dram("q_norm_gain", [NL, 64])
    kg_d = dram("k_norm_gain", [NL, 64])
    pw_d = dram("pool_w", [NL, 4, 128, 128])
    pb_d = dram("pool_b", [NL, 512])
    psc_d = dram("pool_scale", [NL, 512])
    wa_d = dram("w_a", [NL, 512, D])
    wb_d = dram("w_b", [NL, 512, D])
    wo_d = dram("w_out", [NL, D, D])
    cs_d = dram("cs", [T, 128])
    id_d = dram("ident", [128, 128])
    rc_d = dram("rcnt", [128, 64])
    p2_d = dram("pow2", [128, 2 * (ITER + 1)])
    out_d = dram("out", [NS, 128 * NT, D], kind="ExternalOutput")
    h_d = nc.dram_tensor("hbuf", [NS, T, D], F32).ap()

    def sb(name, shape, dt=F32):
        return stack.enter_context(nc.sbuf_tensor(name, list(shape), dt))

    def ps(name, shape, dt=F32):
        return stack.enter_context(nc.psum_tensor(name, list(shape), dt))

    W_in = sb("W_in", [128, 8, D_IN], BF16)
    W_a = sb("W_a", [128, 4, D], BF16)
    W_b = sb("W_b", [128, 4, D], BF16)
    W_o = sb("W_o", [128, 8, D], BF16)
    W_p = sb("W_p", [128, 4, 128], BF16)
    gq2 = sb("gq2", [128, 128])
    gk2 = sb("gk2", [128, 128])
    pb_bc = sb("pb_bc", [128, 512])
    ng_t = sb("ng_t", [128, 8])
    ident_b = sb("ident_b", [128, 128], BF16)
    rcnt = sb("rcnt_t", [128, 64])
    pow2 = sb("pow2_t", [128, 2 * (ITER + 1)])
    kT = sb("kT", [64, T], BF16)
    kiT = sb("kiT", [64, T], BF16)
    V1 = sb("V1", [128, NTT, 66], BF16)
    score = sb("score", [128, SCW])
    fB = score[:, 0:1024]
    scr = sb("scr", [128, max(T // 2, 1024)])
    maskq = scr[:, :].bitcast(BF16)
    maskT = sb("maskT", [128, NTT, 128], BF16)
    cbb = [scr[:, 256 * k:256 * (k + 1)].bitcast(BF16) for k in range(4)]
    PT = [sb(f"PT{i}", [128, 1024], BF16) for i in range(2)]
    hnA = PT[0]
    rA = PT[1][:, :].bitcast(F32)
    rB = score[:, 1024:1536]
    xa = sb("xa", [128, D])
    xr = sb("xr", [128, D])
    cst = sb("cst", [128, 128])
    hT = [sb(f"hT{i}", [128, 8, 128], BF16) for i in range(3)]
    qT = [sb(f"qT{i}", [64, 8, 128], BF16) for i in range(2)]
    qwT = sb("qwT", [64, 8, 128], BF16)
    fA = sb("fA", [128, D])
    ident_f = fA[:, 0:128]
    gtmp = fA[:, 128:256]
    psc_bc = fA[:, 512:1024]
    bA = sb("bA", [128, D], BF16)
    hn = bA
    bB = bA[:, 512:1024]
    sg = sb("sg", [128, 512], BF16)
    sg2 = sb("sg2", [128, 512], BF16)
    sig = sb("sig", [128, 1024], BF16)
    uT = sb("uT", [128, 4, 144])
    pa = sb("pa", [128, 1, 144])
    pbuf = sb("pbuf", [128, 1, 144])
    plT = sb("plT", [128, 4, 128], BF16)
    gT = sb("gT", [128, 8, 128], BF16)
    cstab = sb("cstab", [128, 128])
    sm = sb("sm", [128, 64])
    wv = sb("wv", [128, 8])
    lo8 = sb("lo8", [128, 8])
    hi8 = sb("hi8", [128, 8])
    Ht = sb("Ht", [128, 2 * (ITER + 1)])
    tcol = sb("tcol", [128, ITER + 2])
    cnts = sb("cnts", [128, ITER + 1])
    dcol = sb("dcol", [128, ITER + 1])
    mhalf = sb("mhalf", [128, 8])
    SSQ, RSTD, SSQ8, R8, SSQK, RK, BND, H0, THR, MNC, RDEN = 0, 1, 2, 10, 18, 19, 20, 21, 22, 23, 24

    psA = ps("psA", [128, 1024])
    psB = ps("psB", [128, 1024])
    psC = ps("psC", [128, 1024])
    ps6 = ps("ps6", [128, 512])
    psT = ps("psT", [128, 8, 128], BF16)
    psT_main = psT
    psT_A = psB[:, 512:1024].bitcast(BF16).rearrange("p (s c) -> p s c", s=8)

    S = Sched(nc, stack, self_sync=self_sync)

    B = {}

    def bf(name):
        if name not in B:
            B[name] = Buf(name)
        return B[name]

    def fBb():
        return [bf("score0"), bf("score1")]

    def mqb():
        return [bf("cbb0"), bf("cbb1"), bf("cbb2"), bf("cbb3"), bf("mq")]

    bank = [bf("psA0"), bf("psA1"), bf("psB0"), bf("psB1")]
    bank_ap = [psA[:, 0:512], psA[:, 512:1024], psB[:, 0:512], psB[:, 512:1024]]
    rr = {"bank": 0, "cb": 0, "stg": 0, "cast": 0, "tile": 0}

    S.dma("ldw0", ident_f[:, :], id_d[:, :], writes=[bf("fA")])
    S.dma("ldw1", rcnt[:, :], rc_d[:, :], writes=[bf("rcnt")])
    S.dma("ldw0", pow2[:, :], p2_d[:, :], writes=[bf("pow2")])
    S.op("dve", lambda e: e.tensor_copy(out=ident_b[:, :], in_=ident_f[:, :]),
         reads=[bf("fA")], writes=[bf("ident_b")])
    S.op("pool", lambda e: e.memset(V1[:, :, 64:66], 1.0), writes=[bf("V1ones")])
    S.op("pool", lambda e: e.memset(mhalf[:, :], -0.5), writes=[bf("mhalf")])

    def load_weights(l):
        wbufs = [bf("W_in"), bf("W_a"), bf("W_b"), bf("W_o"), bf("W_p"), bf("gq2"), bf("gk2"), bf("pb_bc")]
        S.dma("ldw0", ng_t[:, :], ng_d[l].rearrange("(c p) -> p c", p=128), writes=[bf("ng_t")], allow_slow_non_contiguous=True)
        S.dma("ldw1", gtmp[:, 0:64], qg_d[l:l + 1, :].to_broadcast([128, 64]), writes=[bf("fA")])
        S.op("dve", lambda e: e.tensor_scalar(out=gq2[:, 0:64], in0=gtmp[:, 0:64], scalar1=0.125, scalar2=None, op0=ALU.mult),
             reads=[bf("fA")], writes=[bf("gq2")])
        S.op("dve", lambda e: e.tensor_scalar(out=gq2[:, 64:96], in0=gtmp[:, 32:64], scalar1=0.125, scalar2=None, op0=ALU.mult),
             reads=[bf("fA")], writes=[bf("gq2")], partial=True)
        S.op("dve", lambda e: e.tensor_scalar(out=gq2[:, 96:128], in0=gtmp[:, 0:32], scalar1=0.125, scalar2=None, op0=ALU.mult),
             reads=[bf("fA")], writes=[bf("gq2")], partial=True)
        S.dma("ldw1", gtmp[:, 64:128], kg_d[l:l + 1, :].to_broadcast([128, 64]), reads=[], writes=[bf("fA")])
        S.op("dve", lambda e: e.tensor_copy(out=gk2[:, 0:64], in_=gtmp[:, 64:128]), reads=[bf("fA")], writes=[bf("gk2")])
        S.op("dve", lambda e: e.tensor_copy(out=gk2[:, 64:96], in_=gtmp[:, 96:128]), reads=[bf("fA")], writes=[bf("gk2")], partial=True)
        S.op("dve", lambda e: e.tensor_copy(out=gk2[:, 96:128], in_=gtmp[:, 64:96]), reads=[bf("fA")], writes=[bf("gk2")], partial=True)
        S.dma("ldw0", psc_bc[:, :], psc_d[l:l + 1, :].to_broadcast([128, 512]), writes=[bf("fA")])
        S.dma("ldw1", pb_bc[:, :], pb_d[l:l + 1, :].to_broadcast([128, 512]), writes=[bf("pb_bc")])
        S.op("dve", lambda e: e.tensor_tensor(out=pb_bc[:, :], in0=pb_bc[:, :], in1=psc_bc[:, :], op=ALU.mult),
             reads=[bf("pb_bc"), bf("fA")], writes=[bf("pb_bc")])

        def all_sc(i=None):
            if i is None:
                return [bf(f"score{k}") for k in range((SCW + 511) // 512)]
            return [bf(f"score{k}") for k in range(4 * i, 4 * i + 4)]

        def stage(src_ap, width, consume, shape3=None):
            i = rr["stg"] % 2
            rr["stg"] += 1
            sbuf_ap = score[:, i * 2048:i * 2048 + width]
            if shape3 is not None:
                sbuf_ap = sbuf_ap.rearrange("p (a b) -> p a b", a=shape3[0])
            b = bf(f"stg{i}")
            S.dma(f"ldw{i}", sbuf_ap, src_ap, writes=[b] + all_sc(i))
            consume(i, b)

        def cast_eng():
            e = ("dve", "act")[rr["cast"] % 2]
            rr["cast"] += 1
            return e

        def scaled_cast(eng, out_ap, in_ap, scale, reads, writes, partial=True):
            if eng == "act":
                S.op("act", lambda e: e.activation(out=out_ap, in_=in_ap, func=AF.Copy, scale=scale),
                     reads=reads, writes=writes, partial=partial)
            else:
                S.op(eng, lambda e: e.tensor_scalar(out=out_ap, in0=in_ap, scalar1=scale, scalar2=None, op0=ALU.mult),
                     reads=reads, writes=writes, partial=partial)

        pieces = [(0, 2048), (2048, 4096), (4096, D_IN)]
        for kc in range(8):
            for (c0, c1) in pieces:
                def consume(i, b, kc=kc, c0=c0, c1=c1):
                    eng = cast_eng()
                    scaled_cast(eng, W_in[:, kc, c0:c1], score[:, i * 2048:i * 2048 + (c1 - c0)],
                                ng_t[:, kc:kc + 1], [b, bf("ng_t")] + all_sc(i), [bf("W_in")])
                stage(win_d[l, kc * 128:(kc + 1) * 128, c0:c1], c1 - c0, consume)
        for (dst, src, nchunk, name) in ((W_a, wa_d, 4, "W_a"), (W_b, wb_d, 4, "W_b"), (W_o, wo_d, 8, "W_o")):
            for cc in range(0, nchunk, 2):
                def consume(i, b, dst=dst, cc=cc, name=name):
                    eng = cast_eng()
                    src_ap = score[:, i * 2048:(i + 1) * 2048].rearrange("p (c n) -> p c n", c=2)
                    scaled_cast(eng, dst[:, cc:cc + 2, :], src_ap, 0.5, [b] + all_sc(i), [bf(name)])
                stage(src[l, cc * 128:(cc + 2) * 128, :].rearrange("(c p) n -> p c n", p=128), 2048, consume, shape3=(2, 1024))
        def consume(i, b):
            src_ap = score[:, i * 2048:i * 2048 + 512]
            S.op("dve", lambda e: e.tensor_tensor(out=W_p[:, :, :].rearrange("p g d -> p (g d)"), in0=src_ap, in1=psc_bc[:, :], op=ALU.mult),
                 reads=[b, bf("fA")] + all_sc(i), writes=[bf("W_p")])
        stage(pw_d[l].rearrange("g c d -> c g d"), 512, consume, shape3=(4, 128))

    def next_bank():
        i = rr["bank"] % 4
        rr["bank"] += 1
        return bank[i], bank_ap[i]

    def proj_tok(c0, c1, n, bb, bap, hTs, hTb):
        for kc in range(8):
            S.op("pe", lambda e, kc=kc: e.matmul(bap[:n, 0:c1 - c0], lhsT=hTs[:, kc, :n], rhs=W_in[:, kc, c0:c1],
                                                 start=(kc == 0), stop=(kc == 7)),
                 reads=[hTb, bf("W_in")], writes=[bb], partial=(kc > 0))

    def rope_from_psum(pap, n, H, cos_ap, sin_ap, bb, tb, eng_add="pool"):
        W = H * 64
        p3 = pap[:n, 0:W].rearrange("p (h d) -> p h d", h=H)
        a3 = rA[:n, 0:W].rearrange("p (h d) -> p h d", h=H)
        b3 = rB[:n, 0:W].rearrange("p (h d) -> p h d", h=H)
        S.op("dve", lambda e: e.tensor_tensor(out=a3, in0=p3, in1=bc_mid(cos_ap, H), op=ALU.mult),
             reads=[bb, tb], writes=[bf("PT1")])
        S.op("dve", lambda e: e.tensor_tensor(out=b3[:, :, 0:32], in0=p3[:, :, 32:64], in1=bc_mid(sin_ap[:, 0:32], H), op=ALU.mult),
             reads=[bb, tb], writes=[bf("score2")])
        S.op("dve", lambda e: e.tensor_tensor(out=b3[:, :, 32:64], in0=p3[:, :, 0:32], in1=bc_mid(sin_ap[:, 32:64], H), op=ALU.mult),
             reads=[bb, tb], writes=[bf("score2")], partial=True, noself=True)
        S.op(eng_add, lambda e: e.tensor_tensor(out=rA[:n, 0:W], in0=rA[:n, 0:W], in1=rB[:n, 0:W], op=ALU.add),
             reads=[bf("PT1"), bf("score2")], writes=[bf("PT1")])

    def rstd_from_ssq(col_in, col_out, width, n, inv_n):
        S.op("pool", lambda e: e.tensor_scalar(out=sm[:n, col_in:col_in + width], in0=sm[:n, col_in:col_in + width],
                                               scalar1=inv_n, scalar2=EPS, op0=ALU.mult, op1=ALU.add),
             reads=[bf(f"sm{col_in}")], writes=[bf(f"sm{col_in}")])
        S.op("pool", lambda e: e.tensor_tensor(out=sm[:n, col_out:col_out + width], in0=sm[:n, col_in:col_in + width],
                                               in1=mhalf[:n, 0:width], op=ALU.pow),
             reads=[bf(f"sm{col_in}"), bf("mhalf")], writes=[bf(f"sm{col_out}")])

    def transposes_to(src_ap_fn, nblk, n, rows_out, dst_ap, dst_buf, src_bufs, evac="act", stage=None):
        dst_bufs = dst_buf if isinstance(dst_buf, (list, tuple)) else [dst_buf]
        psT, psTb = stage if stage is not None else (psT_main, bf("psT"))
        for i in range(nblk):
            S.op("pe", lambda e, i=i: e.transpose(out=psT[:rows_out, i, :n], in_=src_ap_fn(i), identity=ident_b[:n, :n]),
                 reads=src_bufs + [bf("ident_b")], writes=[psTb], partial=(i > 0))
        if evac == "act":
            S.op("act", lambda e: e.activation(out=dst_ap, in_=psT[:rows_out, 0:nblk, :n], func=AF.Copy),
                 reads=[psTb], writes=dst_bufs)
        else:
            S.op("dve", lambda e: e.tensor_copy(out=dst_ap, in_=psT[:rows_out, 0:nblk, :n]),
                 reads=[psTb], writes=dst_bufs)

    def tile_geom(j):
        n = 16 if j == 0 else 128
        pos0 = 0 if j == 0 else 16 + 128 * (j - 1)
        return n, pos0

    def x_src(l, s, j):
        n, pos0 = tile_geom(j)
        if l == 0:
            return (meta_d[:, :] if j == 0 else x_d[s, 128 * (j - 1):128 * j, :]), []
        return h_d[s, pos0:pos0 + n, :], [bf(f"h{s}_{j}")]

    def load_xa(l, s, j):
        n, pos0 = tile_geom(j)
        src_ap, rb = x_src(l, s, j)
        S.dma("ldx", xa[:n, :], src_ap, reads=rb, writes=[bf("xa")])
        S.dma("ldc", cst[:n, :], cs_d[pos0:pos0 + n, :], writes=[bf("cst")])

    def load_xr(l, s, j):
        n, pos0 = tile_geom(j)
        src_ap, rb = x_src(l, s, j)
        S.dma("ldr", xr[:n, :], src_ap, reads=rb, writes=[bf("xr")])

    def phase_a(l, s, j):
        n, pos0 = tile_geom(j)
        sl = j % 2
        hTs, hTb = hT[j % 3], bf(f"hT{j % 3}")
        bx = bf("xa")
        bcs = bf("cst")
        S.op("act", lambda e: e.activation(out=hnA[:n, :], in_=xa[:n, :], func=AF.Square, accum_out=sm[:n, SSQ:SSQ + 1]),
             reads=[bx], writes=[bf("PT0"), bf(f"sm{SSQ}")])
        rstd_from_ssq(SSQ, RSTD, 1, n, 1.0 / D)
        S.op("act", lambda e: e.activation(out=hnA[:n, :], in_=xa[:n, :], func=AF.Copy, scale=sm[:n, RSTD:RSTD + 1]),
             reads=[bx, bf(f"sm{RSTD}")], writes=[bf("PT0")])
        transposes_to(lambda i: hnA[:n, i * 128:(i + 1) * 128], 8, n, 128, hTs[:, :, :n], hTb, [bf("PT0")], evac="act", stage=(psT_A, bf("psB1")))

        bb, bap = bf("psB1"), psB[:, 512:1024]
        proj_tok(C_K, C_K + 128, n, bb, bap, hTs, hTb)
        S.op("act", lambda e: e.activation(out=hnA[:n, 512:1024][:n, 0:64], in_=bap[:n, 0:64], func=AF.Square, accum_out=sm[:n, SSQK:SSQK + 1]),
             reads=[bb], writes=[bf("PT0"), bf(f"sm{SSQK}")])
        rstd_from_ssq(SSQK, RK, 1, n, 1.0 / 64)
        S.op("dve", lambda e: e.scalar_tensor_tensor(out=cstab[:n, :], in0=cst[:n, :], scalar=sm[:n, RK:RK + 1], in1=gk2[:n, :],
                                                     op0=ALU.mult, op1=ALU.mult),
             reads=[bcs, bf(f"sm{RK}"), bf("gk2")], writes=[bf("cstab")])
        rope_from_psum(bap, n, 1, cstab[:n, 0:64], cstab[:n, 64:128], bb, bf("cstab"), eng_add="dve")
        S.op("dve", lambda e: e.tensor_copy(out=hnA[:n, 512:1024][:n, 0:64], in_=rA[:n, 0:64]), reads=[bf("PT1")], writes=[bf("PT0")])
        S.op("act", lambda e: e.activation(out=V1[:n, j, 0:64], in_=bap[:n, 64:128], func=AF.Copy),
             reads=[bb], writes=[bf("V1")], partial=True)
        transposes_to(lambda i: hnA[:n, 512:1024][:n, 0:64], 1, n, 64, kT[:, pos0:pos0 + n].unsqueeze(1), bf("kT"), [bf("PT0")], evac="act", stage=(psT_A, bf("psB1")))

        bb, bap = bf("psB1"), psB[:, 512:1024]
        proj_tok(C_KI, C_KI + 72, n, bb, bap, hTs, hTb)
        S.op("act", lambda e: e.activation(out=wv[:n, :], in_=bap[:n, 64:72], func=AF.Copy), reads=[bb], writes=[bf("wv")])
        S.op("pool", lambda e: e.tensor_scalar(out=lo8[:n, :], in0=wv[:n, :], scalar1=0.0, scalar2=-BIG, op0=ALU.is_lt, op1=ALU.mult),
             reads=[bf("wv")], writes=[bf("lo8")])
        S.op("pool", lambda e: e.tensor_scalar(out=hi8[:n, :], in0=wv[:n, :], scalar1=0.0, scalar2=BIG, op0=ALU.is_ge, op1=ALU.mult),
             reads=[bf("wv")], writes=[bf("hi8")])
        rope_from_psum(bap, n, 1, cst[:n, 0:64], cst[:n, 64:128], bb, bcs, eng_add="dve")
        S.op("dve", lambda e: e.tensor_copy(out=hnA[:n, 512:1024][:n, 0:64], in_=rA[:n, 0:64]), reads=[bf("PT1")], writes=[bf("PT0")])
        transposes_to(lambda i: hnA[:n, 512:1024][:n, 0:64], 1, n, 64, kiT[:, pos0:pos0 + n].unsqueeze(1), bf("kiT"), [bf("PT0")], evac="act", stage=(psT_A, bf("psB1")))

        if l == LASTL and j == 0:
            return
        if j >= 2:
            bb, bap = bf("psB1"), psB[:, 512:1024]
            proj_tok(C_QI, C_QI + 512, n, bb, bap, hTs, hTb)
            rope_from_psum(bap, n, 8, cst[:n, 0:64], cst[:n, 64:128], bb, bcs)
            S.op("pool", lambda e: e.tensor_tensor(out=hnA[:n, 0:512].rearrange("p (h d) -> p h d", h=8),
                                                   in0=rA[:n, 0:512].rearrange("p (h d) -> p h d", h=8),
                                                   in1=bc_last(wv[:n, :], 64), op=ALU.mult),
                 reads=[bf("PT1"), bf("wv")], writes=[bf("PT0")])
            transposes_to(lambda i: hnA[:n, i * 64:(i + 1) * 64], 8, n, 64, qwT[:, :, :n], bf("qwT"), [bf("PT0")], evac="act", stage=(psT_A, bf("psB1")))

        bb, bap = bf("psB1"), psB[:, 512:1024]
        proj_tok(C_Q, C_Q + 512, n, bb, bap, hTs, hTb)
        for h in range(8):
            S.op("act", lambda e, h=h: e.activation(out=hnA[:n, 512:1024][:n, h * 64:(h + 1) * 64], in_=bap[:n, h * 64:(h + 1) * 64], func=AF.Square,
                                                    accum_out=sm[:n, SSQ8 + h:SSQ8 + h + 1]),
                 reads=[bb], writes=[bf("PT0"), bf(f"sm{SSQ8}")], partial=(h > 0), noself=(h > 0))
        rstd_from_ssq(SSQ8, R8, 8, n, 1.0 / 64)
        S.op("dve", lambda e: e.tensor_tensor(out=cstab[:n, :], in0=cst[:n, :], in1=gq2[:n, :], op=ALU.mult),
             reads=[bcs, bf("gq2")], writes=[bf("cstab")])
        rope_from_psum(bap, n, 8, cstab[:n, 0:64], cstab[:n, 64:128], bb, bf("cstab"))
        S.op("pool", lambda e: e.tensor_tensor(out=hnA[:n, 0:512].rearrange("p (h d) -> p h d", h=8),
                                               in0=rA[:n, 0:512].rearrange("p (h d) -> p h d", h=8),
                                               in1=bc_last(sm[:n, R8:R8 + 8], 64), op=ALU.mult),
             reads=[bf("PT1"), bf(f"sm{R8}")], writes=[bf("PT0")])
        transposes_to(lambda i: hnA[:n, i * 64:(i + 1) * 64], 8, n, 64, qT[sl][:, :, :n], bf(f"qT{sl}"), [bf("PT0")], evac="act", stage=(psT_A, bf("psB1")))

    def phase_i(l, s, j):
        n, pos0 = tile_geom(j)
        N = pos0 + n
        nblk = (N + 511) // 512
        accs = [(bf("psC0"), psC[:, 0:512]), (bf("psC1"), psC[:, 512:1024])]
        for b in range(nblk):
            c0 = b * 512
            c1 = min(N, c0 + 512)
            wd = c1 - c0
            bs = bf(f"score{b}")
            accb, acc = accs[b % 2]
            ybanks = []

            def emit_y(h):
                bb, bap = next_bank()
                ybanks.append((bb, bap))
                S.op("pe", lambda e, h=h, bap=bap: e.matmul(bap[:n, 0:wd], lhsT=qwT[:, h, :n], rhs=kiT[:, c0:c1], start=True, stop=True),
                     reads=[bf("qwT"), bf("kiT")], writes=[bb])

            def emit_acc(h):
                bb, bap = ybanks[h]
                ci = rr["cb"] % 4
                rr["cb"] += 1
                S.op("dve", lambda e, h=h, bap=bap, ci=ci: e.tensor_scalar(out=cbb[ci][:n, 0:wd], in0=bap[:n, 0:wd],
                                                                            scalar1=lo8[:n, h:h + 1], scalar2=hi8[:n, h:h + 1],
                                                                            op0=ALU.max, op1=ALU.min),
                     reads=[bb, bf("lo8"), bf("hi8")], writes=[bf(f"cbb{ci}")])
                S.op("pe", lambda e, h=h, ci=ci: e.matmul(acc[:n, 0:wd], lhsT=ident_b[:n, :n], rhs=cbb[ci][:n, 0:wd],
                                                          start=(h == 0), stop=(h == 7)),
                     reads=[bf(f"cbb{ci}"), bf("ident_b")], writes=[accb], partial=(h > 0))

            emit_y(0)
            for h in range(8):
                if h + 1 < 8:
                    emit_y(h + 1)
                emit_acc(h)
            S.op("act", lambda e, acc=acc: e.activation(out=score[:n, c0:c1], in_=acc[:n, 0:wd], func=AF.Copy),
                 reads=[accb], writes=[bs])

    def phase_t(l, s, j):
        n, pos0 = tile_geom(j)
        N = pos0 + n
        nblk = (N + 511) // 512
        sc_all = [bf(f"score{b}") for b in range(nblk)]
        S.op("dve", lambda e: e.tensor_scalar(out=maskq[:n, 0:N], in0=score[:n, 0:N], scalar1=1.0, scalar2=None,
                                              op0=ALU.mult, op1=ALU.max, accum_out=sm[:n, BND:BND + 1]),
             reads=sc_all, writes=mqb() + [bf("bnd")])
        S.op("dve", lambda e: e.tensor_scalar(out=maskq[:n, 0:N], in0=score[:n, 0:N], scalar1=1.0, scalar2=None,
                                              op0=ALU.mult, op1=ALU.min, accum_out=sm[:n, MNC:MNC + 1]),
             reads=sc_all, writes=mqb() + [bf("mnc")])
        S.op("dve", lambda e: e.memset(score[0:64, N - 64:N], -BIG), reads=[],
             writes=[sc_all[k] for k in range((N - 64) // 512, (N - 1) // 512 + 1)], partial=True)
        S.op("dve", lambda e: e.tensor_tensor(out=sm[:n, H0:H0 + 1], in0=sm[:n, BND:BND + 1], in1=sm[:n, MNC:MNC + 1], op=ALU.subtract),
             reads=[bf("bnd"), bf("mnc")], writes=[bf("h0")])
        S.op("dve", lambda e: e.tensor_scalar(out=sm[:n, H0:H0 + 1], in0=sm[:n, H0:H0 + 1], scalar1=0.50005, scalar2=1e-6,
                                              op0=ALU.mult, op1=ALU.add),
             reads=[bf("h0")], writes=[bf("h0")])
        S.op("dve", lambda e: e.tensor_scalar(out=Ht[:n, :], in0=pow2[:n, :], scalar1=sm[:n, H0:H0 + 1], scalar2=None, op0=ALU.mult),
             reads=[bf("h0"), bf("pow2")], writes=[bf("Ht")])
        S.op("dve", lambda e: e.tensor_tensor(out=tcol[:n, 0:1], in0=sm[:n, BND:BND + 1], in1=sm[:n, MNC:MNC + 1], op=ALU.add),
             reads=[bf("bnd"), bf("mnc")], writes=[bf("tcol")])
        S.op("dve", lambda e: e.tensor_scalar(out=tcol[:n, 0:1], in0=tcol[:n, 0:1], scalar1=0.5, scalar2=None, op0=ALU.mult),
             reads=[bf("tcol")], writes=[bf("tcol")])
        for i in range(ITER):
            S.op("dve", lambda e, i=i: e.tensor_scalar(out=maskq[:n, 0:N], in0=score[:n, 0:N], scalar1=tcol[:n, i:i + 1], scalar2=None,
                                                       op0=ALU.is_ge, op1=ALU.add, accum_out=cnts[:n, i:i + 1]),
                 reads=sc_all + [bf("tcol")], writes=mqb() + [bf("cnts")])
            S.op("dve", lambda e, i=i: e.tensor_scalar(out=dcol[:n, i:i + 1], in0=cnts[:n, i:i + 1], scalar1=float(KTOP) - 0.5,
                                                       scalar2=Ht[:n, ITER + 2 + i:ITER + 3 + i], op0=ALU.is_ge, op1=ALU.mult),
                 reads=[bf("cnts"), bf("Ht")], writes=[bf("dcol")])
            S.op("dve", lambda e, i=i: e.tensor_scalar(out=tcol[:n, i + 1:i + 2], in0=dcol[:n, i:i + 1], scalar1=tcol[:n, i:i + 1],
                                                       scalar2=Ht[:n, i + 1:i + 2], op0=ALU.add, op1=ALU.subtract),
                 reads=[bf("dcol"), bf("tcol"), bf("Ht")], writes=[bf("tcol")])
        S.op("dve", lambda e: e.tensor_tensor(out=sm[:n, THR:THR + 1], in0=tcol[:n, ITER:ITER + 1], in1=Ht[:n, ITER:ITER + 1], op=ALU.subtract),
             reads=[bf("tcol"), bf("Ht")], writes=[bf("thr")])
        S.op("dve", lambda e: e.tensor_scalar(out=maskq[:n, 0:N], in0=score[:n, 0:N], scalar1=sm[:n, THR:THR + 1], scalar2=None, op0=ALU.is_ge),
             reads=sc_all + [bf("thr")], writes=mqb())

    def phase_m(l, s, j):
        n, pos0 = tile_geom(j)
        nkt = j + 1
        for g0 in range(0, nkt, 8):
            g1 = min(nkt, g0 + 8)
            for kt in range(g0, g1):
                nk, kp = tile_geom(kt)
                S.op("pe", lambda e, kt=kt, nk=nk, kp=kp, g0=g0: e.transpose(out=psT[:nk, kt - g0, :n], in_=maskq[:n, kp:kp + nk],
                                                                               identity=ident_b[:n, :n]),
                     reads=mqb() + [bf("ident_b")], writes=[bf("psT")], partial=(kt > g0))
            if g0 == 0:
                S.op("act", lambda e: e.activation(out=maskT[:16, 0, :n], in_=psT[:16, 0, :n], func=AF.Copy),
                     reads=[bf("psT")], writes=[bf("maskT")])
                if g1 > 1:
                    S.op("act", lambda e, g1=g1: e.activation(out=maskT[:, 1:g1, :n], in_=psT[:, 1:g1, :n], func=AF.Copy),
                         reads=[bf("psT")], writes=[bf("maskT")], partial=True)
            else:
                S.op("act", lambda e, g0=g0, g1=g1: e.activation(out=maskT[:, g0:g1, :n], in_=psT[:, 0:g1 - g0, :n], func=AF.Copy),
                     reads=[bf("psT")], writes=[bf("maskT")], partial=True)

    def phase_s(l, s, j):
        n, pos0 = tile_geom(j)
        nkt = j + 1
        sl = j % 2
        qTs, qTb = qT[sl], bf(f"qT{sl}")
        load_xr(l, s, j)
        if j == 0:
            S.op("pool", lambda e: e.memset(maskT[:16, 0, :16], 1.0), writes=[bf("maskT")])
        elif j == 1:
            S.op("pool", lambda e: e.memset(maskT[:, 0:2, :], 1.0), writes=[bf("maskT")])
            S.op("pool", lambda e: e.memset(maskT[64:128, 1, 0:64], 0.0), writes=[bf("maskT")], partial=True)
        pairs = [(psA, bf("psA0"), bf("psA1")), (psB, bf("psB0"), bf("psB1"))]

        def qk_part(kt):
            nk, kp = tile_geom(kt)
            pr, b0, b1 = pairs[kt % 2]
            P = PT[kt % 2]
            bP = bf(f"PT{kt % 2}")
            for hh, bb in ((0, b0), (1, b1)):
                S.op("pe", lambda e, hh=hh, pr=pr, nk=nk, kp=kp: e.matmul(pr[:nk, hh * 512:hh * 512 + 4 * n].rearrange("p (h q) -> p h q", h=4),
                                                                          lhsT=kT[:, kp:kp + nk], rhs=qTs[:, hh * 4:(hh + 1) * 4, :n],
                                                                          start=True, stop=True),
                     reads=[bf("kT"), qTb], writes=[bb])
            for hh, bb in ((0, b0), (1, b1)):
                S.op("act", lambda e, hh=hh, pr=pr, nk=nk, P=P: e.activation(out=P[:nk, hh * 512:hh * 512 + 4 * n],
                                                                              in_=pr[:nk, hh * 512:hh * 512 + 4 * n], func=AF.Exp),
                     reads=[bb], writes=[bP], partial=(hh > 0), noself=(hh > 0))
            for hh in (0, 1):
                S.op("pool", lambda e, hh=hh, nk=nk, P=P, kt=kt: e.tensor_tensor(
                    out=P[:nk, hh * 512:hh * 512 + 4 * n].rearrange("p (h q) -> p h q", h=4),
                    in0=P[:nk, hh * 512:hh * 512 + 4 * n].rearrange("p (h q) -> p h q", h=4),
                    in1=bc_mid(maskT[:nk, kt, :n], 4), op=ALU.mult),
                    reads=[bP, bf("maskT")], writes=[bP], partial=(hh > 0), noself=(hh > 0))

        def pv_part(kt):
            nk, kp = tile_geom(kt)
            P = PT[kt % 2]
            bP = bf(f"PT{kt % 2}")
            for h in range(8):
                hh, hl = divmod(h, 4)
                acc = psC[:n, hh * 512 + hl * 65:hh * 512 + hl * 65 + 65]
                S.op("pe", lambda e, h=h, hh=hh, hl=hl, acc=acc, nk=nk, P=P, kt=kt: e.matmul(
                    acc, lhsT=P[:nk, hh * 512 + hl * n:hh * 512 + (hl + 1) * n], rhs=V1[:nk, kt, 0:65],
                    start=(kt == 0 and hl == 0), stop=(kt == nkt - 1 and hl == 3), skip_group_check=True),
                    reads=[bP, bf("V1"), bf("V1ones")], writes=[bf(f"psC{hh}")], partial=not (kt == 0 and hl == 0))

        qk_part(0)
        for kt in range(nkt):
            if kt + 1 < nkt:
                qk_part(kt + 1)
            pv_part(kt)
        for hh in (0, 1):
            acc3 = psC[:n, hh * 512:hh * 512 + 260].rearrange("p (h e) -> p h e", h=4)
            S.op("dve", lambda e, hh=hh, acc3=acc3: e.reciprocal(out=sm[:n, RDEN + hh * 4:RDEN + hh * 4 + 4].unsqueeze(2), in_=acc3[:, :, 64:65]),
                 reads=[bf(f"psC{hh}")], writes=[bf("rden")], partial=(hh > 0), noself=(hh > 0))
        for hh in (0, 1):
            acc3 = psC[:n, hh * 512:hh * 512 + 260].rearrange("p (h e) -> p h e", h=4)
            S.op("dve", lambda e, hh=hh, acc3=acc3: e.tensor_tensor(out=fA[:n, hh * 256:(hh + 1) * 256].rearrange("p (h d) -> p h d", h=4),
                                                                    in0=acc3[:, :, 0:64], in1=bc_last(sm[:n, RDEN + hh * 4:RDEN + hh * 4 + 4], 64),
                                                                    op=ALU.mult),
                 reads=[bf(f"psC{hh}"), bf("rden")], writes=[bf("fA")], partial=(hh > 0), noself=(hh > 0))

    def u_and_halo(l, s, j, pool_only_halo=False):
        n, pos0 = tile_geom(j)
        hTs, hTb = hT[j % 3], bf(f"hT{j % 3}")
        for g in range(4):
            for kc in range(8):
                S.op("pe", lambda e, g=g, kc=kc: e.matmul(ps6[:, g * 128:g * 128 + n], lhsT=W_in[:, kc, C_U + g * 128:C_U + (g + 1) * 128],
                                                          rhs=hTs[:, kc, :n], start=(kc == 0), stop=(kc == 7)),
                     reads=[hTb, bf("W_in")], writes=[bf("ps6")], partial=(g > 0 or kc > 0))
        S.op("act", lambda e: e.activation(out=uT[:, :, 16:16 + n], in_=ps6[:, :].rearrange("p (g t) -> p g t", g=4)[:, :, 0:n], func=AF.Copy),
             reads=[bf("ps6")], writes=[bf("uT")], partial=True)

    def halo_copy(n):
        S.op("pool", lambda e: e.tensor_copy(out=uT[:, :, 0:16], in_=uT[:, :, n:n + 16]), reads=[bf("uT")], writes=[bf("uT")], partial=True)

    def gate2silu(c0, n, hTs, hTb, bb, bap, sg=None, sgn="sg"):
        proj_tok(c0, c0 + 512, n, bb, bap, hTs, hTb)
        S.op("act", lambda e: e.activation(out=sg[:n, :], in_=bap[:n, 0:512], func=AF.Tanh, scale=0.5),
             reads=[bb], writes=[bf(sgn)])
        S.op("dve", lambda e: e.scalar_tensor_tensor(out=sg[:n, :], in0=sg[:n, :], scalar=1.0, in1=bap[:n, 0:512],
                                                     op0=ALU.add, op1=ALU.mult),
             reads=[bb, bf(sgn)], writes=[bf(sgn)])

    def merge_tanh(c0, n, hTs, hTb):
        for g, (bb, bap) in enumerate(((bf("psB0"), psB[:, 0:512]), (bf("psB0"), psB[:, 0:512]))):
            proj_tok(c0 + g * 512, c0 + (g + 1) * 512, n, bb, bap, hTs, hTb)
            S.op("act", lambda e, g=g, bap=bap: e.activation(out=sig[:n, g * 512:(g + 1) * 512], in_=bap[:n, :], func=AF.Tanh, scale=0.5),
                 reads=[bb], writes=[bf("sig")], partial=(g > 0))

    def phase_c(l, s, j):
        n, pos0 = tile_geom(j)
        last = (l == LASTL)
        hTs, hTb = hT[j % 3], bf(f"hT{j % 3}")
        u_and_halo(l, s, j)
        gate2silu(C_GA, n, hTs, hTb, bf("psC0"), psC[:, 0:512], sg=sg, sgn="sg")
        gate2silu(C_GB, n, hTs, hTb, bf("psC1"), psC[:, 512:1024], sg=sg2, sgn="sg2")
        S.op("pool", lambda e: e.tensor_tensor(out=bA[:n, 0:512], in0=fA[:n, 0:512], in1=sg[:n, :], op=ALU.mult),
             reads=[bf("fA"), bf("sg")], writes=[bf("bA")])
        transposes_to(lambda i: bA[:n, i * 128:(i + 1) * 128], 4, n, 128, gT[:, 0:4, :n], bf("gTa"), [bf("bA")], evac="act")
        for hf, bb in ((0, bf("psA0")), (1, bf("psA1"))):
            for c in range(4):
                S.op("pe", lambda e, hf=hf, c=c: e.matmul(psA[:n, hf * 512:(hf + 1) * 512], lhsT=gT[:, c, :n], rhs=W_a[:, c, hf * 512:(hf + 1) * 512],
                                                          start=(c == 0), stop=(c == 3)),
                     reads=[bf("gTa"), bf("W_a")], writes=[bb], partial=(c > 0))
        for g, w in enumerate(POOL_W):
            cur = uT
            curb = bf("uT")
            gi = g
            m = 1
            k = 0
            while m < w:
                lo_t = 16 - (w - 2 * m)
                nxt, nxtb = (pa, bf("pa")) if k % 2 == 0 else (pbuf, bf("pbuf"))
                S.op("pool", lambda e, gi=gi, cur=cur, nxt=nxt, lo_t=lo_t, m=m: e.tensor_tensor(
                    out=nxt[:, 0, lo_t:16 + n], in0=cur[:, gi, lo_t:16 + n], in1=cur[:, gi, lo_t - m:16 + n - m], op=ALU.add),
                    reads=[curb], writes=[nxtb])
                cur, curb, gi = nxt, nxtb, 0
                m *= 2
                k += 1
            if j == 0:
                S.op("pool", lambda e, g=g, cur=cur: e.tensor_tensor(out=cur[:, 0, 16:32], in0=cur[:, 0, 16:32], in1=rcnt[:, g * 16:(g + 1) * 16], op=ALU.mult),
                     reads=[curb, bf("rcnt")], writes=[curb])
                S.op("pool", lambda e, g=g, cur=cur: e.tensor_tensor(out=plT[:, g, 0:16], in0=cur[:, 0, 16:32], in1=uT[:, g, 16:32], op=ALU.subtract),
                     reads=[curb, bf("uT")], writes=[bf("plT")], partial=(g > 0))
            else:
                S.op("dve", lambda e, g=g, cur=cur, w=w: e.scalar_tensor_tensor(out=plT[:, g, 0:n], in0=cur[:, 0, 16:16 + n], scalar=1.0 / w,
                                                                                in1=uT[:, g, 16:16 + n], op0=ALU.mult, op1=ALU.subtract),
                     reads=[curb, bf("uT")], writes=[bf("plT")], partial=(g > 0))
        halo_copy(n)
        merge_tanh(C_M, n, hTs, hTb)
        S.op("dve", lambda e: e.scalar_tensor_tensor(out=fA[:n, :], in0=sig[:n, :], scalar=1.0, in1=psA[:n, :], op0=ALU.add, op1=ALU.mult),
             reads=[bf("sig"), bf("psA0"), bf("psA1")], writes=[bf("fA")])
        for g in range(4):
            S.op("pe", lambda e, g=g: e.matmul(ps6[:n, g * 128:(g + 1) * 128], lhsT=plT[:, g, :n], rhs=W_p[:, g, :], start=True, stop=True),
                 reads=[bf("plT"), bf("W_p")], writes=[bf("ps6")], partial=(g > 0))
        S.op("dve", lambda e: e.tensor_tensor(out=fB[:n, 0:512], in0=ps6[:n, :], in1=pb_bc[:n, :], op=ALU.add),
             reads=[bf("ps6"), bf("pb_bc")], writes=fBb())
        S.op("pool", lambda e: e.tensor_tensor(out=bA[:n, 512:1024], in0=fB[:n, 0:512], in1=sg2[:n, :], op=ALU.mult),
             reads=fBb() + [bf("sg2")], writes=[bf("bA2")])
        transposes_to(lambda i: bA[:n, 512 + i * 128:512 + (i + 1) * 128], 4, n, 128, gT[:, 4:8, :n], bf("gTb"), [bf("bA2")], evac="act")
        for hf, bb in ((0, bf("psA0")), (1, bf("psA1"))):
            for c in range(4):
                S.op("pe", lambda e, hf=hf, c=c: e.matmul(psA[:n, hf * 512:(hf + 1) * 512], lhsT=gT[:, 4 + c, :n], rhs=W_b[:, c, hf * 512:(hf + 1) * 512],
                                                          start=(c == 0), stop=(c == 3)),
                     reads=[bf("gTb"), bf("W_b")], writes=[bb], partial=(c > 0))
        merge_tanh(C_M + 1024, n, hTs, hTb)
        S.op("dve", lambda e: e.scalar_tensor_tensor(out=fB[:n, :], in0=sig[:n, :], scalar=1.0, in1=psA[:n, :], op0=ALU.add, op1=ALU.mult),
             reads=[bf("sig"), bf("psA0"), bf("psA1")], writes=fBb())
        S.op("pool", lambda e: e.tensor_tensor(out=bA[:n, :], in0=fA[:n, :], in1=fB[:n, :], op=ALU.add),
             reads=[bf("fA")] + fBb(), writes=[bf("bA"), bf("bA2")])
        transposes_to(lambda i: bA[:n, i * 128:(i + 1) * 128], 8, n, 128, gT[:, :, :n], [bf("gTa"), bf("gTb")], [bf("bA"), bf("bA2")], evac="act")
        for hf, bb in ((0, bf("psC0")), (1, bf("psC1"))):
            for c in range(8):
                S.op("pe", lambda e, hf=hf, c=c: e.matmul(psC[:n, hf * 512:(hf + 1) * 512], lhsT=gT[:, c, :n], rhs=W_o[:, c, hf * 512:(hf + 1) * 512],
                                                          start=(c == 0), stop=(c == 7)),
                     reads=[bf("gTa"), bf("gTb"), bf("W_o")], writes=[bb], partial=(c > 0))
        S.op("dve", lambda e: e.tensor_tensor(out=xr[:n, :], in0=psC[:n, :], in1=xr[:n, :], op=ALU.add),
             reads=[bf("psC0"), bf("psC1"), bf("xr")], writes=[bf("xr")])
        if last:
            if j >= 1:
                S.dma("st", out_d[s, 128 * (j - 1):128 * j, :], xr[:n, :], reads=[bf("xr")], writes=[bf(f"o{s}_{j}")])
        else:
            S.dma("st", h_d[s, pos0:pos0 + n, :], xr[:n, :], reads=[bf("xr")], writes=[bf(f"h{s}_{j}")])

    for l in range(NL):
        S.label = f"W:{l}"
        load_weights(l)
        for s in range(NS):
            S.op("pool", lambda e: e.memset(uT[:, :, 0:16], 0.0), writes=[bf("uT")])
            load_xa(l, s, 0)
            S.label = f"A:{l}:{s}:0"
            phase_a(l, s, 0)
            if NTT > 1:
                load_xa(l, s, 1)
                S.label = f"A:{l}:{s}:1"
                phase_a(l, s, 1)
                if NTT > 2:
                    load_xa(l, s, 2)
            for j in range(NTT):
                jn = j + 1
                if jn < NTT and jn >= 2:
                    S.label = f"I:{l}:{s}:{jn}"
                    phase_i(l, s, jn)
                    S.label = f"T:{l}:{s}:{jn}"
                    phase_t(l, s, jn)
                skip = (l == LASTL and j == 0)
                if not skip:
                    S.label = f"S:{l}:{s}:{j}"
                    phase_s(l, s, j)
                if jn < NTT and jn >= 2:
                    S.label = f"M:{l}:{s}:{jn}"
                    phase_m(l, s, jn)

                def c_part(j=j, skip=skip):
                    S.label = f"C:{l}:{s}:{j}"
                    if not skip:
                        phase_c(l, s, j)
                    else:
                        u_and_halo(l, s, j)
                        halo_copy(16)

                def a_part(j=j):
                    if j + 2 < NTT:
                        S.label = f"A:{l}:{s}:{j + 2}"
                        phase_a(l, s, j + 2)

                lc = S.capture(c_part)
                la = S.capture(a_part)
                S.replay(lc, la)
                if j + 3 < NTT:
                    load_xa(l, s, j + 3)
    S.finish()
    S.emit()
    nc._sched_labels = S.labels
    return nc, stack


def host_constants(T, ITER):
    inv_freq = 1.0 / (10000.0 ** (np.arange(0, 64, 2, dtype=np.float32) / 64.0))
    ang = np.arange(T, dtype=np.float32)[:, None] * inv_freq[None, :]
    ang = np.concatenate([ang, ang], axis=-1).astype(np.float32)
    cos = np.cos(ang).astype(np.float32)
    sin = np.sin(ang).astype(np.float32)
    sin_s = sin.copy()
    sin_s[:, :32] = -sin_s[:, :32]
    cs = np.concatenate([cos, sin_s], axis=1).astype(np.float32)
    ident = np.eye(128, dtype=np.float32)
    rc = np.zeros((128, 64), np.float32)
    for g, w in enumerate(POOL_W):
        for t in range(16):
            rc[:, g * 16 + t] = 1.0 / min(t + 1, w)
    p2 = np.zeros((128, 2 * (ITER + 1)), np.float32)
    for i in range(ITER + 1):
        p2[:, i] = 2.0 ** (-i)
        p2[:, ITER + 1 + i] = 2.0 ** (1 - i)
    return cs, ident, rc, p2


_CACHE = {}


def kernel(x, meta_tokens, norm_gain, w_in, q_norm_gain, k_norm_gain, pool_w, pool_b, pool_scale,
           w_branch_a, w_branch_b, w_out):
    x = np.asarray(x, dtype=np.float32)
    Bsz, Sq, _ = x.shape
    NL = int(np.asarray(norm_gain).shape[0])
    n_cores = 8
    NS = Bsz // n_cores
    NT = Sq // 128
    ITER = 18
    KTOP = min(256, Sq // 4)
    key = (NS, NT, NL, KTOP, ITER)
    if key not in _CACHE:
        _CACHE[key] = build_program(NS, NT, NL, KTOP, ITER)
    nc, _stack = _CACHE[key]
    T = 16 + 128 * NT
    cs, ident, rc, p2 = host_constants(T, ITER)
    f = lambda a: np.ascontiguousarray(np.asarray(a, dtype=np.float32))
    shared = {
        "meta": f(meta_tokens), "norm_gain": f(norm_gain), "w_in": f(w_in), "q_norm_gain": f(q_norm_gain),
        "k_norm_gain": f(k_norm_gain), "pool_w": f(pool_w), "pool_b": f(pool_b), "pool_scale": f(pool_scale),
        "w_a": f(w_branch_a), "w_b": f(w_branch_b), "w_out": f(w_out),
        "cs": cs, "ident": ident, "rcnt": rc, "pow2": p2,
    }
    in_maps = []
    for c in range(n_cores):
        m = dict(shared)
        m["x"] = np.ascontiguousarray(x[c * NS:(c + 1) * NS])
        in_maps.append(m)
    res = run_bass_kernel_spmd(nc, in_maps, core_ids=list(range(n_cores)))
    out = np.concatenate([np.asarray(r["out"]) for r in res.results], axis=0)
    return out.astype(np.float32)
```

```python
import numpy as np
from contextlib import ExitStack
import concourse.bass as bass
import concourse.mybir as mybir
from concourse.bass_utils import run_bass_kernel_spmd

F32 = mybir.dt.float32
BF16 = mybir.dt.bfloat16
AF = mybir.ActivationFunctionType
ALU = mybir.AluOpType
AX = mybir.AxisListType

D = 1024
D_IN = 4808
C_Q, C_K, C_V, C_GA, C_U, C_GB, C_QI, C_KI, C_WI, C_M = 0, 512, 576, 640, 1152, 1664, 2176, 2688, 2752, 2760
EPS = 1e-6
BIG = 1.0e30
POOL_W = (2, 4, 8, 16)


class Buf:
    __slots__ = ("name", "w", "r")

    def __init__(self, name=""):
        self.name = name
        self.w = {}
        self.r = {}


class _Rec:
    def __init__(self):
        self.call = None

    def __getattr__(self, name):
        def m(*a, **k):
            self.call = (name, a, k)
            return self
        return m


class Sched:
    ENGS = ("pe", "act", "dve", "pool", "sp")

    def __init__(self, nc, stack, self_sync=True):
        self.nc = nc
        self.stack = stack
        self.streams = {e: [] for e in self.ENGS}
        self.seen = {e: {} for e in self.ENGS}
        self.sems = {}
        self.cnt = {}
        self.self_sync = self_sync
        self.label = ""
        self._cap = None
        self.labels = {e: [] for e in self.ENGS}
        for e in ("pe", "act", "dve", "pool"):
            self.sems[e] = stack.enter_context(nc.semaphore("s_" + e))
            self.cnt[e] = 0

    def _chan(self, name):
        if name not in self.sems:
            self.sems[name] = self.stack.enter_context(self.nc.semaphore("c_" + name))
            self.cnt[name] = 0

    @staticmethod
    def _deps(reads, writes):
        deps = {}
        for b in reads:
            for k, v in b.w.items():
                if deps.get(k, 0) < v:
                    deps[k] = v
            if b.name.startswith("ps"):
                for k, v in b.r.items():
                    if deps.get(k, 0) < v:
                        deps[k] = v
        for b in writes:
            for k, v in b.w.items():
                if deps.get(k, 0) < v:
                    deps[k] = v
            for k, v in b.r.items():
                if deps.get(k, 0) < v:
                    deps[k] = v
        return deps

    def _waits(self, eng, deps, noself=False):
        waits = []
        seen = self.seen[eng]
        for k, v in deps.items():
            if k == eng and (eng == "pe" or noself or not self.self_sync):
                continue
            if seen.get(k, 0) >= v:
                continue
            seen[k] = v
            waits.append((k, v))
        return waits

    def capture(self, thunk):
        self._cap = []
        try:
            thunk()
        finally:
            lst, self._cap = self._cap, None
        return lst

    def replay(self, *lists):
        lists = [l for l in lists if l]
        pos = [0] * len(lists)
        tot = sum(len(l) for l in lists)
        for _ in range(tot):
            best, bi = None, -1
            for i, l in enumerate(lists):
                if pos[i] < len(l):
                    f = pos[i] / len(l)
                    if best is None or f < best:
                        best, bi = f, i
            kind, a, k, lab = lists[bi][pos[bi]]
            pos[bi] += 1
            self.label = lab
            if kind == "op":
                self.op(*a, **k)
            else:
                self.dma(*a, **k)

    def op(self, eng, fn, reads=(), writes=(), partial=False, noself=False):
        if getattr(self, "_cap", None) is not None:
            r = _Rec()
            fn(r)
            call = r.call
            self._cap.append(("op", (eng, (lambda e, call=call: getattr(e, call[0])(*call[1], **call[2]))),
                              dict(reads=list(reads), writes=list(writes), partial=partial, noself=noself), self.label))
            return
        waits = self._waits(eng, self._deps(reads, writes), noself)
        self.cnt[eng] += 1
        s = self.cnt[eng]
        r = _Rec()
        fn(r)
        self.streams[eng].append((waits, r.call, eng, 1))
        self.labels[eng].append((self.label, 2 if r.call[2].get("accum_out") is not None else 1))
        for b in writes:
            if not partial:
                b.w = {}
                b.r = {}
            b.w[eng] = s
        for b in reads:
            b.r[eng] = s

    def dma(self, ch, out_ap, in_ap, reads=(), writes=(), q="sp", **kw):
        if getattr(self, "_cap", None) is not None:
            self._cap.append(("dma", (ch, out_ap, in_ap), dict(reads=list(reads), writes=list(writes), q=q, **kw), self.label))
            return
        self._chan(ch)
        deps = self._deps(reads, writes)
        if self.cnt[ch] > 0:
            deps[ch] = max(deps.get(ch, 0), self.cnt[ch])
        waits = self._waits(q, deps)
        self.cnt[ch] += 16
        s = self.cnt[ch]
        self.streams[q].append((waits, ("dma_start", (), dict(out=out_ap, in_=in_ap, **kw)), ch, 16))
        for b in writes:
            b.w = {ch: s}
            b.r = {}
        for b in reads:
            b.r[ch] = s

    def finish(self):
        waits = []
        for k, v in self.cnt.items():
            if k in ("pe", "act", "dve", "pool"):
                continue
            if v > 0:
                waits.append((k, v))
        self.streams["sp"].append((waits, None, None, 0))

    def emit(self):
        nc = self.nc
        sems = self.sems

        def runner(stream):
            def body(e):
                for waits, fn, key, inc in stream:
                    for k, v in waits:
                        e.wait_ge(sems[k], v)
                    if fn is not None:
                        name, a, k = fn
                        getattr(e, name)(*a, **k).then_inc(sems[key], inc)
            return body

        with nc.Block() as block:
            block.tensor(runner(self.streams["pe"]))
            block.scalar(runner(self.streams["act"]))
            block.vector(runner(self.streams["dve"]))
            block.gpsimd(runner(self.streams["pool"]))
            block.sync(runner(self.streams["sp"]))


def bc_mid(ap2d, reps):
    p, f = ap2d.shape
    return ap2d.unsqueeze(1).to_broadcast([p, reps, f])


def bc_last(ap2d, reps):
    p, g = ap2d.shape
    return ap2d.unsqueeze(2).to_broadcast([p, g, reps])


def build_program(NS, NT, NL, KTOP, ITER=22, self_sync=True, dbg=False):
    T = 16 + 128 * NT
    NTT = NT + 1
    SCW = max(T, 4096)
    LASTL = NL - 1 if not dbg else 10 ** 6
    nc = bass.Bass("TRN2", target_bir_lowering=False)
    stack = ExitStack()

    def dram(name, shape, dt=F32, kind="ExternalInput"):
        return nc.dram_tensor(name, list(shape), dt, kind=kind).ap()

    x_d = dram("x", [NS, 128 * NT, D])
    meta_d = dram("meta", [16, D])
    ng_d = dram("norm_gain", [NL, D])
    win_d = dram("w_in", [NL, D, D_IN])
    qg_d = dram("q_norm_gain", [NL, 64])
    kg_d = dram("k_norm_gain", [NL, 64])
    pw_d = dram("pool_w", [NL, 4, 128, 128])
    pb_d = dram("pool_b", [NL, 512])
    psc_d = dram("pool_scale", [NL, 512])
    wa_d = dram("w_a", [NL, 512, D])
    wb_d = dram("w_b", [NL, 512, D])
    wo_d = dram("w_out", [NL, D, D])
    cs_d = dram("cs", [T, 128])
    id_d = dram("ident", [128, 128])
    rc_d = dram("rcnt", [128, 64])
    p2_d = dram("pow2", [128, 2 * (ITER + 1)])
    out_d = dram("out", [NS, 128 * NT, D], kind="ExternalOutput")
    h_d = nc.dram_tensor("hbuf", [NS, T, D], F32).ap()

    def sb(name, shape, dt=F32):
        return stack.enter_context(nc.sbuf_tensor(name, list(shape), dt))

    def ps(name, shape, dt=F32):
        return stack.enter_context(nc.psum_tensor(name, list(shape), dt))

    W_in = sb("W_in", [128, 8, D_IN], BF16)
    W_a = sb("W_a", [128, 4, D], BF16)
    W_b = sb("W_b", [128, 4, D], BF16)
    W_o = sb("W_o", [128, 8, D], BF16)
    W_p = sb("W_p", [128, 4, 128], BF16)
    gq2 = sb("gq2", [128, 128])
    gk2 = sb("gk2", [128, 128])
    pb_bc = sb("pb_bc", [128, 512])
    ng_t = sb("ng_t", [128, 8])
    ident_b = sb("ident_b", [128, 128], BF16)
    rcnt = sb("rcnt_t", [128, 64])
    pow2 = sb("pow2_t", [128, 2 * (ITER + 1)])
    kT = sb("kT", [64, T], BF16)
    kiT = sb("kiT", [64, T], BF16)
    V1 = sb("V1", [128, NTT, 66], BF16)
    score = sb("score", [128, SCW])
    fB = score[:, 0:1024]
    scr = sb("scr", [128, max(T // 2, 1024)])
    maskq = scr[:, :].bitcast(BF16)
    maskT = sb("maskT", [128, NTT, 128], BF16)
    cbb = [scr[:, 256 * k:256 * (k + 1)].bitcast(BF16) for k in range(4)]
    PT = [sb(f"PT{i}", [128, 1024], BF16) for i in range(2)]
    hnA = PT[0]
    rA = PT[1][:, :].bitcast(F32)
    rB = score[:, 1024:1536]
    xa = sb("xa", [128, D])
    xr = sb("xr", [128, D])
    cst = sb("cst", [128, 128])
    hT = [sb(f"hT{i}", [128, 8, 128], BF16) for i in range(3)]
    qT = [sb(f"qT{i}", [64, 8, 128], BF16) for i in range(2)]
    qwT = sb("qwT", [64, 8, 128], BF16)
    fA = sb("fA", [128, D])
    ident_f = fA[:, 0:128]
    gtmp = fA[:, 128:256]
    psc_bc = fA[:, 512:1024]
    bA = sb("bA", [128, D], BF16)
    hn = bA
    bB = bA[:, 512:1024]
    sg = sb("sg", [128, 512], BF16)
    sg2 = sb("sg2", [128, 512], BF16)
    sig = sb("sig", [128, 1024], BF16)
    uT = sb("uT", [128, 4, 144])
    pa = sb("pa", [128, 1, 144])
    pbuf = sb("pbuf", [128, 1, 144])
    plT = sb("plT", [128, 4, 128], BF16)
    gT = sb("gT", [128, 8, 128], BF16)
    cstab = sb("cstab", [128, 128])
    sm = sb("sm", [128, 64])
    wv = sb("wv", [128, 8])
    lo8 = sb("lo8", [128, 8])
    hi8 = sb("hi8", [128, 8])
    Ht = sb("Ht", [128, 2 * (ITER + 1)])
    tcol = sb("tcol", [128, ITER + 2])
    cnts = sb("cnts", [128, ITER + 1])
    dcol = sb("dcol", [128, ITER + 1])
    mhalf = sb("mhalf", [128, 8])
    SSQ, RSTD, SSQ8, R8, SSQK, RK, BND, H0, THR, MNC, RDEN = 0, 1, 2, 10, 18, 19, 20, 21, 22, 23, 24

    psA = ps("psA", [128, 1024])
    psB = ps("psB", [128, 1024])
    psC = ps("psC", [128, 1024])
    ps6 = ps("ps6", [128, 512])
    psT = ps("psT", [128, 8, 128], BF16)
    psT_main = psT
    psT_A = psB[:, 512:1024].bitcast(BF16).rearrange("p (s c) -> p s c", s=8)

    S = Sched(nc, stack, self_sync=self_sync)

    B = {}

    def bf(name):
        if name not in B:
            B[name] = Buf(name)
        return B[name]

    def fBb():
        return [bf("score0"), bf("score1")]

    def mqb():
        return [bf("cbb0"), bf("cbb1"), bf("cbb2"), bf("cbb3"), bf("mq")]

    bank = [bf("psA0"), bf("psA1"), bf("psB0"), bf("psB1")]
    bank_ap = [psA[:, 0:512], psA[:, 512:1024], psB[:, 0:512], psB[:, 512:1024]]
    rr = {"bank": 0, "cb": 0, "stg": 0, "cast": 0, "tile": 0}

    S.dma("ldw0", ident_f[:, :], id_d[:, :], writes=[bf("fA")])
    S.dma("ldw1", rcnt[:, :], rc_d[:, :], writes=[bf("rcnt")])
    S.dma("ldw0", pow2[:, :], p2_d[:, :], writes=[bf("pow2")])
    S.op("dve", lambda e: e.tensor_copy(out=ident_b[:, :], in_=ident_f[:, :]),
         reads=[bf("fA")], writes=[bf("ident_b")])
    S.op("pool", lambda e: e.memset(V1[:, :, 64:66], 1.0), writes=[bf("V1ones")])
    S.op("pool", lambda e: e.memset(mhalf[:, :], -0.5), writes=[bf("mhalf")])

    def load_weights(l):
        wbufs = [bf("W_in"), bf("W_a"), bf("W_b"), bf("W_o"), bf("W_p"), bf("gq2"), bf("gk2"), bf("pb_bc")]
        S.dma("ldw0", ng_t[:, :], ng_d[l].rearrange("(c p) -> p c", p=128), writes=[bf("ng_t")], allow_slow_non_contiguous=True)
        S.dma("ldw1", gtmp[:, 0:64], qg_d[l:l + 1, :].to_broadcast([128, 64]), writes=[bf("fA")])
        S.op("dve", lambda e: e.tensor_scalar(out=gq2[:, 0:64], in0=gtmp[:, 0:64], scalar1=0.125, scalar2=None, op0=ALU.mult),
             reads=[bf("fA")], writes=[bf("gq2")])
        S.op("dve", lambda e: e.tensor_scalar(out=gq2[:, 64:96], in0=gtmp[:, 32:64], scalar1=0.125, scalar2=None, op0=ALU.mult),
             reads=[bf("fA")], writes=[bf("gq2")], partial=True)
        S.op("dve", lambda e: e.tensor_scalar(out=gq2[:, 96:128], in0=gtmp[:, 0:32], scalar1=0.125, scalar2=None, op0=ALU.mult),
             reads=[bf("fA")], writes=[bf("gq2")], partial=True)
        S.dma("ldw1", gtmp[:, 64:128], kg_d[l:l + 1, :].to_broadcast([128, 64]), reads=[], writes=[bf("fA")])
        S.op("dve", lambda e: e.tensor_copy(out=gk2[:, 0:64], in_=gtmp[:, 64:128]), reads=[bf("fA")], writes=[bf("gk2")])
        S.op("dve", lambda e: e.tensor_copy(out=gk2[:, 64:96], in_=gtmp[:, 96:128]), reads=[bf("fA")], writes=[bf("gk2")], partial=True)
        S.op("dve", lambda e: e.tensor_copy(out=gk2[:, 96:128], in_=gtmp[:, 64:96]), reads=[bf("fA")], writes=[bf("gk2")], partial=True)
        S.dma("ldw0", psc_bc[:, :], psc_d[l:l + 1, :].to_broadcast([128, 512]), writes=[bf("fA")])
        S.dma("ldw1", pb_bc[:, :], pb_d[l:l + 1, :].to_broadcast([128, 512]), writes=[bf("pb_bc")])
        S.op("dve", lambda e: e.tensor_tensor(out=pb_bc[:, :], in0=pb_bc[:, :], in1=psc_bc[:, :], op=ALU.mult),
             reads=[bf("pb_bc"), bf("fA")], writes=[bf("pb_bc")])

        def all_sc(i=None):
            if i is None:
                return [bf(f"score{k}") for k in range((SCW + 511) // 512)]
            return [bf(f"score{k}") for k in range(4 * i, 4 * i + 4)]

        def stage(src_ap, width, consume, shape3=None):
            i = rr["stg"] % 2
            rr["stg"] += 1
            sbuf_ap = score[:, i * 2048:i * 2048 + width]
            if shape3 is not None:
                sbuf_ap = sbuf_ap.rearrange("p (a b) -> p a b", a=shape3[0])
            b = bf(f"stg{i}")
            S.dma(f"ldw{i}", sbuf_ap, src_ap, writes=[b] + all_sc(i))
            consume(i, b)

        def cast_eng():
            e = ("dve", "act")[rr["cast"] % 2]
            rr["cast"] += 1
            return e

        def scaled_cast(eng, out_ap, in_ap, scale, reads, writes, partial=True):
            if eng == "act":
                S.op("act", lambda e: e.activation(out=out_ap, in_=in_ap, func=AF.Copy, scale=scale),
                     reads=reads, writes=writes, partial=partial)
            else:
                S.op(eng, lambda e: e.tensor_scalar(out=out_ap, in0=in_ap, scalar1=scale, scalar2=None, op0=ALU.mult),
                     reads=reads, writes=writes, partial=partial)

        pieces = [(0, 2048), (2048, 4096), (4096, D_IN)]
        for kc in range(8):
            for (c0, c1) in pieces:
                def consume(i, b, kc=kc, c0=c0, c1=c1):
                    eng = cast_eng()
                    scaled_cast(eng, W_in[:, kc, c0:c1], score[:, i * 2048:i * 2048 + (c1 - c0)],
                                ng_t[:, kc:kc + 1], [b, bf("ng_t")] + all_sc(i), [bf("W_in")])
                stage(win_d[l, kc * 128:(kc + 1) * 128, c0:c1], c1 - c0, consume)
        for (dst, src, nchunk, name) in ((W_a, wa_d, 4, "W_a"), (W_b, wb_d, 4, "W_b"), (W_o, wo_d, 8, "W_o")):
            for cc in range(0, nchunk, 2):
                def consume(i, b, dst=dst, cc=cc, name=name):
                    eng = cast_eng()
                    src_ap = score[:, i * 2048:(i + 1) * 2048].rearrange("p (c n) -> p c n", c=2)
                    scaled_cast(eng, dst[:, cc:cc + 2, :], src_ap, 0.5, [b] + all_sc(i), [bf(name)])
                stage(src[l, cc * 128:(cc + 2) * 128, :].rearrange("(c p) n -> p c n", p=128), 2048, consume, shape3=(2, 1024))
        def consume(i, b):
            src_ap = score[:, i * 2048:i * 2048 + 512]
            S.op("dve", lambda e: e.tensor_tensor(out=W_p[:, :, :].rearrange("p g d -> p (g d)"), in0=src_ap, in1=psc_bc[:, :], op=ALU.mult),
                 reads=[b, bf("fA")] + all_sc(i), writes=[bf("W_p")])
        stage(pw_d[l].rearrange("g c d -> c g d"), 512, consume, shape3=(4, 128))

    def next_bank():
        i = rr["bank"] % 4
        rr["bank"] += 1
        return bank[i], bank_ap[i]

    def proj_tok(c0, c1, n, bb, bap, hTs, hTb):
        for kc in range(8):
            S.op("pe", lambda e, kc=kc: e.matmul(bap[:n, 0:c1 - c0], lhsT=hTs[:, kc, :n], rhs=W_in[:, kc, c0:c1],
                                                 start=(kc == 0), stop=(kc == 7)),
                 reads=[hTb, bf("W_in")], writes=[bb], partial=(kc > 0))

    def rope_from_psum(pap, n, H, cos_ap, sin_ap, bb, tb, eng_add="pool"):
        W = H * 64
        p3 = pap[:n, 0:W].rearrange("p (h d) -> p h d", h=H)
        a3 = rA[:n, 0:W].rearrange("p (h d) -> p h d", h=H)
        b3 = rB[:n, 0:W].rearrange("p (h d) -> p h d", h=H)
        S.op("dve", lambda e: e.tensor_tensor(out=a3, in0=p3, in1=bc_mid(cos_ap, H), op=ALU.mult),
             reads=[bb, tb], writes=[bf("PT1")])
        S.op("dve", lambda e: e.tensor_tensor(out=b3[:, :, 0:32], in0=p3[:, :, 32:64], in1=bc_mid(sin_ap[:, 0:32], H), op=ALU.mult),
             reads=[bb, tb], writes=[bf("score2")])
        S.op("dve", lambda e: e.tensor_tensor(out=b3[:, :, 32:64], in0=p3[:, :, 0:32], in1=bc_mid(sin_ap[:, 32:64], H), op=ALU.mult),
             reads=[bb, tb], writes=[bf("score2")], partial=True, noself=True)
        S.op(eng_add, lambda e: e.tensor_tensor(out=rA[:n, 0:W], in0=rA[:n, 0:W], in1=rB[:n, 0:W], op=ALU.add),
             reads=[bf("PT1"), bf("score2")], writes=[bf("PT1")])

    def rstd_from_ssq(col_in, col_out, width, n, inv_n):
        S.op("dve", lambda e: e.tensor_scalar(out=sm[:n, col_in:col_in + width], in0=sm[:n, col_in:col_in + width],
                                              scalar1=inv_n, scalar2=EPS, op0=ALU.mult, op1=ALU.add),
             reads=[bf(f"sm{col_in}")], writes=[bf(f"sm{col_in}")])
        S.op("pool", lambda e: e.tensor_tensor(out=sm[:n, col_out:col_out + width], in0=sm[:n, col_in:col_in + width],
                                               in1=mhalf[:n, 0:width], op=ALU.pow),
             reads=[bf(f"sm{col_in}"), bf("mhalf")], writes=[bf(f"sm{col_out}")])

    def transposes_to(src_ap_fn, nblk, n, rows_out, dst_ap, dst_buf, src_bufs, evac="act", stage=None):
        dst_bufs = dst_buf if isinstance(dst_buf, (list, tuple)) else [dst_buf]
        psT, psTb = stage if stage is not None else (psT_main, bf("psT"))
        for i in range(nblk):
            S.op("pe", lambda e, i=i: e.transpose(out=psT[:rows_out, i, :n], in_=src_ap_fn(i), identity=ident_b[:n, :n]),
                 reads=src_bufs + [bf("ident_b")], writes=[psTb], partial=(i > 0))
        if evac == "act":
            S.op("act", lambda e: e.activation(out=dst_ap, in_=psT[:rows_out, 0:nblk, :n], func=AF.Copy),
                 reads=[psTb], writes=dst_bufs)
        else:
            S.op("dve", lambda e: e.tensor_copy(out=dst_ap, in_=psT[:rows_out, 0:nblk, :n]),
                 reads=[psTb], writes=dst_bufs)

    def tile_geom(j):
        n = 16 if j == 0 else 128
        pos0 = 0 if j == 0 else 16 + 128 * (j - 1)
        return n, pos0

    def x_src(l, s, j):
        n, pos0 = tile_geom(j)
        if l == 0:
            return (meta_d[:, :] if j == 0 else x_d[s, 128 * (j - 1):128 * j, :]), []
        return h_d[s, pos0:pos0 + n, :], [bf(f"h{s}_{j}")]

    def load_xa(l, s, j):
        n, pos0 = tile_geom(j)
        src_ap, rb = x_src(l, s, j)
        S.dma("ldx", xa[:n, :], src_ap, reads=rb, writes=[bf("xa")])
        S.dma("ldc", cst[:n, :], cs_d[pos0:pos0 + n, :], writes=[bf("cst")])

    def load_xr(l, s, j):
        n, pos0 = tile_geom(j)
        src_ap, rb = x_src(l, s, j)
        S.dma("ldr", xr[:n, :], src_ap, reads=rb, writes=[bf("xr")])

    def phase_a(l, s, j):
        n, pos0 = tile_geom(j)
        sl = j % 2
        hTs, hTb = hT[j % 3], bf(f"hT{j % 3}")
        bx = bf("xa")
        bcs = bf("cst")
        S.op("act", lambda e: e.activation(out=hnA[:n, :], in_=xa[:n, :], func=AF.Square, accum_out=sm[:n, SSQ:SSQ + 1]),
             reads=[bx], writes=[bf("PT0"), bf(f"sm{SSQ}")])
        rstd_from_ssq(SSQ, RSTD, 1, n, 1.0 / D)
        S.op("act", lambda e: e.activation(out=hnA[:n, :], in_=xa[:n, :], func=AF.Copy, scale=sm[:n, RSTD:RSTD + 1]),
             reads=[bx, bf(f"sm{RSTD}")], writes=[bf("PT0")])
        transposes_to(lambda i: hnA[:n, i * 128:(i + 1) * 128], 8, n, 128, hTs[:, :, :n], hTb, [bf("PT0")], evac="act", stage=(psT_A, bf("psB1")))

        bb, bap = bf("psB1"), psB[:, 512:1024]
        proj_tok(C_K, C_K + 128, n, bb, bap, hTs, hTb)
        S.op("act", lambda e: e.activation(out=hnA[:n, 512:1024][:n, 0:64], in_=bap[:n, 0:64], func=AF.Square, accum_out=sm[:n, SSQK:SSQK + 1]),
             reads=[bb], writes=[bf("PT0"), bf(f"sm{SSQK}")])
        rstd_from_ssq(SSQK, RK, 1, n, 1.0 / 64)
        S.op("dve", lambda e: e.scalar_tensor_tensor(out=cstab[:n, :], in0=cst[:n, :], scalar=sm[:n, RK:RK + 1], in1=gk2[:n, :],
                                                     op0=ALU.mult, op1=ALU.mult),
             reads=[bcs, bf(f"sm{RK}"), bf("gk2")], writes=[bf("cstab")])
        rope_from_psum(bap, n, 1, cstab[:n, 0:64], cstab[:n, 64:128], bb, bf("cstab"), eng_add="dve")
        S.op("dve", lambda e: e.tensor_copy(out=hnA[:n, 512:1024][:n, 0:64], in_=rA[:n, 0:64]), reads=[bf("PT1")], writes=[bf("PT0")])
        S.op("act", lambda e: e.activation(out=V1[:n, j, 0:64], in_=bap[:n, 64:128], func=AF.Copy),
             reads=[bb], writes=[bf("V1")], partial=True)
        transposes_to(lambda i: hnA[:n, 512:1024][:n, 0:64], 1, n, 64, kT[:, pos0:pos0 + n].unsqueeze(1), bf("kT"), [bf("PT0")], evac="act", stage=(psT_A, bf("psB1")))

        bb, bap = bf("psB1"), psB[:, 512:1024]
        proj_tok(C_KI, C_KI + 72, n, bb, bap, hTs, hTb)
        S.op("act", lambda e: e.activation(out=wv[:n, :], in_=bap[:n, 64:72], func=AF.Copy), reads=[bb], writes=[bf("wv")])
        S.op("pool", lambda e: e.tensor_scalar(out=lo8[:n, :], in0=wv[:n, :], scalar1=0.0, scalar2=-BIG, op0=ALU.is_lt, op1=ALU.mult),
             reads=[bf("wv")], writes=[bf("lo8")])
        S.op("pool", lambda e: e.tensor_scalar(out=hi8[:n, :], in0=wv[:n, :], scalar1=0.0, scalar2=BIG, op0=ALU.is_ge, op1=ALU.mult),
             reads=[bf("wv")], writes=[bf("hi8")])
        rope_from_psum(bap, n, 1, cst[:n, 0:64], cst[:n, 64:128], bb, bcs, eng_add="dve")
        S.op("dve", lambda e: e.tensor_copy(out=hnA[:n, 512:1024][:n, 0:64], in_=rA[:n, 0:64]), reads=[bf("PT1")], writes=[bf("PT0")])
        transposes_to(lambda i: hnA[:n, 512:1024][:n, 0:64], 1, n, 64, kiT[:, pos0:pos0 + n].unsqueeze(1), bf("kiT"), [bf("PT0")], evac="act", stage=(psT_A, bf("psB1")))

        if l == LASTL and j == 0:
            return
        if j >= 2:
            bb, bap = bf("psB1"), psB[:, 512:1024]
            proj_tok(C_QI, C_QI + 512, n, bb, bap, hTs, hTb)
            rope_from_psum(bap, n, 8, cst[:n, 0:64], cst[:n, 64:128], bb, bcs)
            S.op("pool", lambda e: e.tensor_tensor(out=hnA[:n, 0:512].rearrange("p (h d) -> p h d", h=8),
                                                   in0=rA[:n, 0:512].rearrange("p (h d) -> p h d", h=8),
                                                   in1=bc_last(wv[:n, :], 64), op=ALU.mult),
                 reads=[bf("PT1"), bf("wv")], writes=[bf("PT0")])
            transposes_to(lambda i: hnA[:n, i * 64:(i + 1) * 64], 8, n, 64, qwT[:, :, :n], bf("qwT"), [bf("PT0")], evac="act", stage=(psT_A, bf("psB1")))

        bb, bap = bf("psB1"), psB[:, 512:1024]
        proj_tok(C_Q, C_Q + 512, n, bb, bap, hTs, hTb)
        for h in range(8):
            S.op("act", lambda e, h=h: e.activation(out=hnA[:n, 512:1024][:n, h * 64:(h + 1) * 64], in_=bap[:n, h * 64:(h + 1) * 64], func=AF.Square,
                                                    accum_out=sm[:n, SSQ8 + h:SSQ8 + h + 1]),
                 reads=[bb], writes=[bf("PT0"), bf(f"sm{SSQ8}")], partial=(h > 0), noself=(h > 0))
        rstd_from_ssq(SSQ8, R8, 8, n, 1.0 / 64)
        S.op("dve", lambda e: e.tensor_tensor(out=cstab[:n, :], in0=cst[:n, :], in1=gq2[:n, :], op=ALU.mult),
             reads=[bcs, bf("gq2")], writes=[bf("cstab")])
        rope_from_psum(bap, n, 8, cstab[:n, 0:64], cstab[:n, 64:128], bb, bf("cstab"))
        S.op("pool", lambda e: e.tensor_tensor(out=hnA[:n, 0:512].rearrange("p (h d) -> p h d", h=8),
                                               in0=rA[:n, 0:512].rearrange("p (h d) -> p h d", h=8),
                                               in1=bc_last(sm[:n, R8:R8 + 8], 64), op=ALU.mult),
             reads=[bf("PT1"), bf(f"sm{R8}")], writes=[bf("PT0")])
        transposes_to(lambda i: hnA[:n, i * 64:(i + 1) * 64], 8, n, 64, qT[sl][:, :, :n], bf(f"qT{sl}"), [bf("PT0")], evac="act", stage=(psT_A, bf("psB1")))

    def phase_i(l, s, j):
        n, pos0 = tile_geom(j)
        N = pos0 + n
        nblk = (N + 511) // 512
        accs = [(bf("psC0"), psC[:, 0:512]), (bf("psC1"), psC[:, 512:1024])]
        for b in range(nblk):
            c0 = b * 512
            c1 = min(N, c0 + 512)
            wd = c1 - c0
            bs = bf(f"score{b}")
            accb, acc = accs[b % 2]
            ybanks = []

            def emit_y(h):
                bb, bap = next_bank()
                ybanks.append((bb, bap))
                S.op("pe", lambda e, h=h, bap=bap: e.matmul(bap[:n, 0:wd], lhsT=qwT[:, h, :n], rhs=kiT[:, c0:c1], start=True, stop=True),
                     reads=[bf("qwT"), bf("kiT")], writes=[bb])

            def emit_acc(h):
                bb, bap = ybanks[h]
                ci = rr["cb"] % 4
                rr["cb"] += 1
                S.op("dve", lambda e, h=h, bap=bap, ci=ci: e.tensor_scalar(out=cbb[ci][:n, 0:wd], in0=bap[:n, 0:wd],
                                                                            scalar1=lo8[:n, h:h + 1], scalar2=hi8[:n, h:h + 1],
                                                                            op0=ALU.max, op1=ALU.min),
                     reads=[bb, bf("lo8"), bf("hi8")], writes=[bf(f"cbb{ci}")])
                S.op("pe", lambda e, h=h, ci=ci: e.matmul(acc[:n, 0:wd], lhsT=ident_b[:n, :n], rhs=cbb[ci][:n, 0:wd],
                                                          start=(h == 0), stop=(h == 7)),
                     reads=[bf(f"cbb{ci}"), bf("ident_b")], writes=[accb], partial=(h > 0))

            emit_y(0)
            for h in range(8):
                if h + 1 < 8:
                    emit_y(h + 1)
                emit_acc(h)
            S.op("act", lambda e, acc=acc: e.activation(out=score[:n, c0:c1], in_=acc[:n, 0:wd], func=AF.Copy),
                 reads=[accb], writes=[bs])

    def phase_t(l, s, j):
        n, pos0 = tile_geom(j)
        N = pos0 + n
        nblk = (N + 511) // 512
        sc_all = [bf(f"score{b}") for b in range(nblk)]
        S.op("dve", lambda e: e.tensor_scalar(out=maskq[:n, 0:N], in0=score[:n, 0:N], scalar1=1.0, scalar2=None,
                                              op0=ALU.mult, op1=ALU.max, accum_out=sm[:n, BND:BND + 1]),
             reads=sc_all, writes=mqb() + [bf("bnd")])
        S.op("dve", lambda e: e.tensor_scalar(out=maskq[:n, 0:N], in0=score[:n, 0:N], scalar1=1.0, scalar2=None,
                                              op0=ALU.mult, op1=ALU.min, accum_out=sm[:n, MNC:MNC + 1]),
             reads=sc_all, writes=mqb() + [bf("mnc")])
        S.op("dve", lambda e: e.memset(score[0:64, N - 64:N], -BIG), reads=[],
             writes=[sc_all[k] for k in range((N - 64) // 512, (N - 1) // 512 + 1)], partial=True)
        S.op("dve", lambda e: e.tensor_tensor(out=sm[:n, H0:H0 + 1], in0=sm[:n, BND:BND + 1], in1=sm[:n, MNC:MNC + 1], op=ALU.subtract),
             reads=[bf("bnd"), bf("mnc")], writes=[bf("h0")])
        S.op("dve", lambda e: e.tensor_scalar(out=sm[:n, H0:H0 + 1], in0=sm[:n, H0:H0 + 1], scalar1=0.50005, scalar2=1e-6,
                                              op0=ALU.mult, op1=ALU.add),
             reads=[bf("h0")], writes=[bf("h0")])
        S.op("dve", lambda e: e.tensor_scalar(out=Ht[:n, :], in0=pow2[:n, :], scalar1=sm[:n, H0:H0 + 1], scalar2=None, op0=ALU.mult),
             reads=[bf("h0"), bf("pow2")], writes=[bf("Ht")])
        S.op("dve", lambda e: e.tensor_tensor(out=tcol[:n, 0:1], in0=sm[:n, BND:BND + 1], in1=sm[:n, MNC:MNC + 1], op=ALU.add),
             reads=[bf("bnd"), bf("mnc")], writes=[bf("tcol")])
        S.op("dve", lambda e: e.tensor_scalar(out=tcol[:n, 0:1], in0=tcol[:n, 0:1], scalar1=0.5, scalar2=None, op0=ALU.mult),
             reads=[bf("tcol")], writes=[bf("tcol")])
        for i in range(ITER):
            S.op("dve", lambda e, i=i: e.tensor_scalar(out=maskq[:n, 0:N], in0=score[:n, 0:N], scalar1=tcol[:n, i:i + 1], scalar2=None,
                                                       op0=ALU.is_ge, op1=ALU.add, accum_out=cnts[:n, i:i + 1]),
                 reads=sc_all + [bf("tcol")], writes=mqb() + [bf("cnts")])
            S.op("dve", lambda e, i=i: e.tensor_scalar(out=dcol[:n, i:i + 1], in0=cnts[:n, i:i + 1], scalar1=float(KTOP) - 0.5,
                                                       scalar2=Ht[:n, ITER + 2 + i:ITER + 3 + i], op0=ALU.is_ge, op1=ALU.mult),
                 reads=[bf("cnts"), bf("Ht")], writes=[bf("dcol")])
            S.op("dve", lambda e, i=i: e.tensor_scalar(out=tcol[:n, i + 1:i + 2], in0=dcol[:n, i:i + 1], scalar1=tcol[:n, i:i + 1],
                                                       scalar2=Ht[:n, i + 1:i + 2], op0=ALU.add, op1=ALU.subtract),
                 reads=[bf("dcol"), bf("tcol"), bf("Ht")], writes=[bf("tcol")])
        S.op("dve", lambda e: e.tensor_tensor(out=sm[:n, THR:THR + 1], in0=tcol[:n, ITER:ITER + 1], in1=Ht[:n, ITER:ITER + 1], op=ALU.subtract),
             reads=[bf("tcol"), bf("Ht")], writes=[bf("thr")])
        S.op("dve", lambda e: e.tensor_scalar(out=maskq[:n, 0:N], in0=score[:n, 0:N], scalar1=sm[:n, THR:THR + 1], scalar2=None, op0=ALU.is_ge),
             reads=sc_all + [bf("thr")], writes=mqb())

    def phase_m(l, s, j):
        n, pos0 = tile_geom(j)
        nkt = j + 1
        for g0 in range(0, nkt, 8):
            g1 = min(nkt, g0 + 8)
            for kt in range(g0, g1):
                nk, kp = tile_geom(kt)
                S.op("pe", lambda e, kt=kt, nk=nk, kp=kp, g0=g0: e.transpose(out=psT[:nk, kt - g0, :n], in_=maskq[:n, kp:kp + nk],
                                                                               identity=ident_b[:n, :n]),
                     reads=mqb() + [bf("ident_b")], writes=[bf("psT")], partial=(kt > g0))
            if g0 == 0:
                S.op("act", lambda e: e.activation(out=maskT[:16, 0, :n], in_=psT[:16, 0, :n], func=AF.Copy),
                     reads=[bf("psT")], writes=[bf("maskT")])
                if g1 > 1:
                    S.op("act", lambda e, g1=g1: e.activation(out=maskT[:, 1:g1, :n], in_=psT[:, 1:g1, :n], func=AF.Copy),
                         reads=[bf("psT")], writes=[bf("maskT")], partial=True)
            else:
                S.op("act", lambda e, g0=g0, g1=g1: e.activation(out=maskT[:, g0:g1, :n], in_=psT[:, 0:g1 - g0, :n], func=AF.Copy),
                     reads=[bf("psT")], writes=[bf("maskT")], partial=True)

    def phase_s(l, s, j):
        n, pos0 = tile_geom(j)
        nkt = j + 1
        sl = j % 2
        qTs, qTb = qT[sl], bf(f"qT{sl}")
        load_xr(l, s, j)
        if j == 0:
            S.op("pool", lambda e: e.memset(maskT[:16, 0, :16], 1.0), writes=[bf("maskT")])
        elif j == 1:
            S.op("pool", lambda e: e.memset(maskT[:, 0:2, :], 1.0), writes=[bf("maskT")])
            S.op("pool", lambda e: e.memset(maskT[64:128, 1, 0:64], 0.0), writes=[bf("maskT")], partial=True)
        pairs = [(psA, bf("psA0"), bf("psA1")), (psB, bf("psB0"), bf("psB1"))]

        def qk_part(kt):
            nk, kp = tile_geom(kt)
            pr, b0, b1 = pairs[kt % 2]
            P = PT[kt % 2]
            bP = bf(f"PT{kt % 2}")
            for hh, bb in ((0, b0), (1, b1)):
                S.op("pe", lambda e, hh=hh, pr=pr, nk=nk, kp=kp: e.matmul(pr[:nk, hh * 512:hh * 512 + 4 * n].rearrange("p (h q) -> p h q", h=4),
                                                                          lhsT=kT[:, kp:kp + nk], rhs=qTs[:, hh * 4:(hh + 1) * 4, :n],
                                                                          start=True, stop=True),
                     reads=[bf("kT"), qTb], writes=[bb])
            for hh, bb in ((0, b0), (1, b1)):
                S.op("act", lambda e, hh=hh, pr=pr, nk=nk, P=P: e.activation(out=P[:nk, hh * 512:hh * 512 + 4 * n],
                                                                              in_=pr[:nk, hh * 512:hh * 512 + 4 * n], func=AF.Exp),
                     reads=[bb], writes=[bP], partial=(hh > 0), noself=(hh > 0))
            for hh in (0, 1):
                S.op("pool", lambda e, hh=hh, nk=nk, P=P, kt=kt: e.tensor_tensor(
                    out=P[:nk, hh * 512:hh * 512 + 4 * n].rearrange("p (h q) -> p h q", h=4),
                    in0=P[:nk, hh * 512:hh * 512 + 4 * n].rearrange("p (h q) -> p h q", h=4),
                    in1=bc_mid(maskT[:nk, kt, :n], 4), op=ALU.mult),
                    reads=[bP, bf("maskT")], writes=[bP], partial=(hh > 0), noself=(hh > 0))

        def pv_part(kt):
            nk, kp = tile_geom(kt)
            P = PT[kt % 2]
            bP = bf(f"PT{kt % 2}")
            for h in range(8):
                hh, hl = divmod(h, 4)
                acc = psC[:n, hh * 512 + hl * 65:hh * 512 + hl * 65 + 65]
                S.op("pe", lambda e, h=h, hh=hh, hl=hl, acc=acc, nk=nk, P=P, kt=kt: e.matmul(
                    acc, lhsT=P[:nk, hh * 512 + hl * n:hh * 512 + (hl + 1) * n], rhs=V1[:nk, kt, 0:65],
                    start=(kt == 0 and hl == 0), stop=(kt == nkt - 1 and hl == 3), skip_group_check=True),
                    reads=[bP, bf("V1"), bf("V1ones")], writes=[bf(f"psC{hh}")], partial=not (kt == 0 and hl == 0))

        qk_part(0)
        for kt in range(nkt):
            if kt + 1 < nkt:
                qk_part(kt + 1)
            pv_part(kt)
        for hh in (0, 1):
            acc3 = psC[:n, hh * 512:hh * 512 + 260].rearrange("p (h e) -> p h e", h=4)
            S.op("dve", lambda e, hh=hh, acc3=acc3: e.reciprocal(out=sm[:n, RDEN + hh * 4:RDEN + hh * 4 + 4].unsqueeze(2), in_=acc3[:, :, 64:65]),
                 reads=[bf(f"psC{hh}")], writes=[bf("rden")], partial=(hh > 0), noself=(hh > 0))
        for hh in (0, 1):
            acc3 = psC[:n, hh * 512:hh * 512 + 260].rearrange("p (h e) -> p h e", h=4)
            S.op("dve", lambda e, hh=hh, acc3=acc3: e.tensor_tensor(out=fA[:n, hh * 256:(hh + 1) * 256].rearrange("p (h d) -> p h d", h=4),
                                                                    in0=acc3[:, :, 0:64], in1=bc_last(sm[:n, RDEN + hh * 4:RDEN + hh * 4 + 4], 64),
                                                                    op=ALU.mult),
                 reads=[bf(f"psC{hh}"), bf("rden")], writes=[bf("fA")], partial=(hh > 0), noself=(hh > 0))

    def u_and_halo(l, s, j, pool_only_halo=False):
        n, pos0 = tile_geom(j)
        hTs, hTb = hT[j % 3], bf(f"hT{j % 3}")
        for g in range(4):
            for kc in range(8):
                S.op("pe", lambda e, g=g, kc=kc: e.matmul(ps6[:, g * 128:g * 128 + n], lhsT=W_in[:, kc, C_U + g * 128:C_U + (g + 1) * 128],
                                                          rhs=hTs[:, kc, :n], start=(kc == 0), stop=(kc == 7)),
                     reads=[hTb, bf("W_in")], writes=[bf("ps6")], partial=(g > 0 or kc > 0))
        S.op("act", lambda e: e.activation(out=uT[:, :, 16:16 + n], in_=ps6[:, :].rearrange("p (g t) -> p g t", g=4)[:, :, 0:n], func=AF.Copy),
             reads=[bf("ps6")], writes=[bf("uT")], partial=True)

    def halo_copy(n):
        S.op("pool", lambda e: e.tensor_copy(out=uT[:, :, 0:16], in_=uT[:, :, n:n + 16]), reads=[bf("uT")], writes=[bf("uT")], partial=True)

    def gate2silu(c0, n, hTs, hTb, bb, bap, sg=None, sgn="sg"):
        proj_tok(c0, c0 + 512, n, bb, bap, hTs, hTb)
        S.op("act", lambda e: e.activation(out=sg[:n, :], in_=bap[:n, 0:512], func=AF.Tanh, scale=0.5),
             reads=[bb], writes=[bf(sgn)])
        S.op("dve", lambda e: e.scalar_tensor_tensor(out=sg[:n, :], in0=sg[:n, :], scalar=1.0, in1=bap[:n, 0:512],
                                                     op0=ALU.add, op1=ALU.mult),
             reads=[bb, bf(sgn)], writes=[bf(sgn)])

    def merge_tanh(c0, n, hTs, hTb):
        for g, (bb, bap) in enumerate(((bf("psB0"), psB[:, 0:512]), (bf("psB0"), psB[:, 0:512]))):
            proj_tok(c0 + g * 512, c0 + (g + 1) * 512, n, bb, bap, hTs, hTb)
            S.op("act", lambda e, g=g, bap=bap: e.activation(out=sig[:n, g * 512:(g + 1) * 512], in_=bap[:n, :], func=AF.Tanh, scale=0.5),
                 reads=[bb], writes=[bf("sig")], partial=(g > 0))

    def phase_c(l, s, j):
        n, pos0 = tile_geom(j)
        last = (l == LASTL)
        hTs, hTb = hT[j % 3], bf(f"hT{j % 3}")
        u_and_halo(l, s, j)
        gate2silu(C_GA, n, hTs, hTb, bf("psC0"), psC[:, 0:512], sg=sg, sgn="sg")
        gate2silu(C_GB, n, hTs, hTb, bf("psC1"), psC[:, 512:1024], sg=sg2, sgn="sg2")
        S.op("pool", lambda e: e.tensor_tensor(out=bA[:n, 0:512], in0=fA[:n, 0:512], in1=sg[:n, :], op=ALU.mult),
             reads=[bf("fA"), bf("sg")], writes=[bf("bA")])
        transposes_to(lambda i: bA[:n, i * 128:(i + 1) * 128], 4, n, 128, gT[:, 0:4, :n], bf("gTa"), [bf("bA")], evac="act")
        for hf, bb in ((0, bf("psA0")), (1, bf("psA1"))):
            for c in range(4):
                S.op("pe", lambda e, hf=hf, c=c: e.matmul(psA[:n, hf * 512:(hf + 1) * 512], lhsT=gT[:, c, :n], rhs=W_a[:, c, hf * 512:(hf + 1) * 512],
                                                          start=(c == 0), stop=(c == 3)),
                     reads=[bf("gTa"), bf("W_a")], writes=[bb], partial=(c > 0))
        for g, w in enumerate(POOL_W):
            cur = uT
            curb = bf("uT")
            gi = g
            m = 1
            k = 0
            while m < w:
                lo_t = 16 - (w - 2 * m)
                nxt, nxtb = (pa, bf("pa")) if k % 2 == 0 else (pbuf, bf("pbuf"))
                S.op("pool", lambda e, gi=gi, cur=cur, nxt=nxt, lo_t=lo_t, m=m: e.tensor_tensor(
                    out=nxt[:, 0, lo_t:16 + n], in0=cur[:, gi, lo_t:16 + n], in1=cur[:, gi, lo_t - m:16 + n - m], op=ALU.add),
                    reads=[curb], writes=[nxtb])
                cur, curb, gi = nxt, nxtb, 0
                m *= 2
                k += 1
            if j == 0:
                S.op("pool", lambda e, g=g, cur=cur: e.tensor_tensor(out=cur[:, 0, 16:32], in0=cur[:, 0, 16:32], in1=rcnt[:, g * 16:(g + 1) * 16], op=ALU.mult),
                     reads=[curb, bf("rcnt")], writes=[curb])
                S.op("pool", lambda e, g=g, cur=cur: e.tensor_tensor(out=plT[:, g, 0:16], in0=cur[:, 0, 16:32], in1=uT[:, g, 16:32], op=ALU.subtract),
                     reads=[curb, bf("uT")], writes=[bf("plT")], partial=(g > 0))
            else:
                S.op("dve", lambda e, g=g, cur=cur, w=w: e.scalar_tensor_tensor(out=plT[:, g, 0:n], in0=cur[:, 0, 16:16 + n], scalar=1.0 / w,
                                                                                in1=uT[:, g, 16:16 + n], op0=ALU.mult, op1=ALU.subtract),
                     reads=[curb, bf("uT")], writes=[bf("plT")], partial=(g > 0))
        halo_copy(n)
        merge_tanh(C_M, n, hTs, hTb)
        S.op("dve", lambda e: e.scalar_tensor_tensor(out=fA[:n, :], in0=sig[:n, :], scalar=1.0, in1=psA[:n, :], op0=ALU.add, op1=ALU.mult),
             reads=[bf("sig"), bf("psA0"), bf("psA1")], writes=[bf("fA")])
        for g in range(4):
            S.op("pe", lambda e, g=g: e.matmul(ps6[:n, g * 128:(g + 1) * 128], lhsT=plT[:, g, :n], rhs=W_p[:, g, :], start=True, stop=True),
                 reads=[bf("plT"), bf("W_p")], writes=[bf("ps6")], partial=(g > 0))
        S.op("dve", lambda e: e.tensor_tensor(out=fB[:n, 0:512], in0=ps6[:n, :], in1=pb_bc[:n, :], op=ALU.add),
             reads=[bf("ps6"), bf("pb_bc")], writes=fBb())
        S.op("pool", lambda e: e.tensor_tensor(out=bA[:n, 512:1024], in0=fB[:n, 0:512], in1=sg2[:n, :], op=ALU.mult),
             reads=fBb() + [bf("sg2")], writes=[bf("bA2")])
        transposes_to(lambda i: bA[:n, 512 + i * 128:512 + (i + 1) * 128], 4, n, 128, gT[:, 4:8, :n], bf("gTb"), [bf("bA2")], evac="act")
        for hf, bb in ((0, bf("psA0")), (1, bf("psA1"))):
            for c in range(4):
                S.op("pe", lambda e, hf=hf, c=c: e.matmul(psA[:n, hf * 512:(hf + 1) * 512], lhsT=gT[:, 4 + c, :n], rhs=W_b[:, c, hf * 512:(hf + 1) * 512],
                                                          start=(c == 0), stop=(c == 3)),
                     reads=[bf("gTb"), bf("W_b")], writes=[bb], partial=(c > 0))
        merge_tanh(C_M + 1024, n, hTs, hTb)
        S.op("dve", lambda e: e.scalar_tensor_tensor(out=fB[:n, :], in0=sig[:n, :], scalar=1.0, in1=psA[:n, :], op0=ALU.add, op1=ALU.mult),
             reads=[bf("sig"), bf("psA0"), bf("psA1")], writes=fBb())
        S.op("pool", lambda e: e.tensor_tensor(out=bA[:n, :], in0=fA[:n, :], in1=fB[:n, :], op=ALU.add),
             reads=[bf("fA")] + fBb(), writes=[bf("bA"), bf("bA2")])
        transposes_to(lambda i: bA[:n, i * 128:(i + 1) * 128], 8, n, 128, gT[:, :, :n], [bf("gTa"), bf("gTb")], [bf("bA"), bf("bA2")], evac="act")
        for hf, bb in ((0, bf("psC0")), (1, bf("psC1"))):
            for c in range(8):
                S.op("pe", lambda e, hf=hf, c=c: e.matmul(psC[:n, hf * 512:(hf + 1) * 512], lhsT=gT[:, c, :n], rhs=W_o[:, c, hf * 512:(hf + 1) * 512],
                                                          start=(c == 0), stop=(c == 7)),
                     reads=[bf("gTa"), bf("gTb"), bf("W_o")], writes=[bb], partial=(c > 0))
        S.op("dve", lambda e: e.tensor_tensor(out=xr[:n, :], in0=psC[:n, :], in1=xr[:n, :], op=ALU.add),
             reads=[bf("psC0"), bf("psC1"), bf("xr")], writes=[bf("xr")])
        if last:
            if j >= 1:
                S.dma("st", out_d[s, 128 * (j - 1):128 * j, :], xr[:n, :], reads=[bf("xr")], writes=[bf(f"o{s}_{j}")])
        else:
            S.dma("st", h_d[s, pos0:pos0 + n, :], xr[:n, :], reads=[bf("xr")], writes=[bf(f"h{s}_{j}")])

    for l in range(NL):
        S.label = f"W:{l}"
        load_weights(l)
        for s in range(NS):
            S.op("pool", lambda e: e.memset(uT[:, :, 0:16], 0.0), writes=[bf("uT")])
            load_xa(l, s, 0)
            S.label = f"A:{l}:{s}:0"
            phase_a(l, s, 0)
            if NTT > 1:
                load_xa(l, s, 1)
                S.label = f"A:{l}:{s}:1"
                phase_a(l, s, 1)
                if NTT > 2:
                    load_xa(l, s, 2)
            for j in range(NTT):
                jn = j + 1
                if jn < NTT and jn >= 2:
                    S.label = f"I:{l}:{s}:{jn}"
                    phase_i(l, s, jn)
                    S.label = f"T:{l}:{s}:{jn}"
                    phase_t(l, s, jn)
                skip = (l == LASTL and j == 0)
                if not skip:
                    S.label = f"S:{l}:{s}:{j}"
                    phase_s(l, s, j)
                def c_part(j=j, skip=skip):
                    S.label = f"C:{l}:{s}:{j}"
                    if not skip:
                        phase_c(l, s, j)
                    else:
                        u_and_halo(l, s, j)
                        halo_copy(16)

                def a_part(j=j):
                    if j + 2 < NTT:
                        S.label = f"A:{l}:{s}:{j + 2}"
                        phase_a(l, s, j + 2)

                lc = S.capture(c_part)
                la = S.capture(a_part)
                S.replay(lc, la)
                if j + 3 < NTT:
                    load_xa(l, s, j + 3)
                if jn < NTT and jn >= 2:
                    S.label = f"M:{l}:{s}:{jn}"
                    phase_m(l, s, jn)
    S.finish()
    S.emit()
    nc._sched_labels = S.labels
    return nc, stack


def host_constants(T, ITER):
    inv_freq = 1.0 / (10000.0 ** (np.arange(0, 64, 2, dtype=np.float32) / 64.0))
    ang = np.arange(T, dtype=np.float32)[:, None] * inv_freq[None, :]
    ang = np.concatenate([ang, ang], axis=-1).astype(np.float32)
    cos = np.cos(ang).astype(np.float32)
    sin = np.sin(ang).astype(np.float32)
    sin_s = sin.copy()
    sin_s[:, :32] = -sin_s[:, :32]
    cs = np.concatenate([cos, sin_s], axis=1).astype(np.float32)
    ident = np.eye(128, dtype=np.float32)
    rc = np.zeros((128, 64), np.float32)
    for g, w in enumerate(POOL_W):
        for t in range(16):
            rc[:, g * 16 + t] = 1.0 / min(t + 1, w)
    p2 = np.zeros((128, 2 * (ITER + 1)), np.float32)
    for i in range(ITER + 1):
        p2[:, i] = 2.0 ** (-i)
        p2[:, ITER + 1 + i] = 2.0 ** (1 - i)
    return cs, ident, rc, p2


_CACHE = {}


def kernel(x, meta_tokens, norm_gain, w_in, q_norm_gain, k_norm_gain, pool_w, pool_b, pool_scale,
           w_branch_a, w_branch_b, w_out):
    x = np.asarray(x, dtype=np.float32)
    Bsz, Sq, _ = x.shape
    NL = int(np.asarray(norm_gain).shape[0])
    n_cores = 8
    NS = Bsz // n_cores
    NT = Sq // 128
    ITER = 18
    KTOP = min(256, Sq // 4)
    key = (NS, NT, NL, KTOP, ITER)
    if key not in _CACHE:
        _CACHE[key] = build_program(NS, NT, NL, KTOP, ITER)
    nc, _stack = _CACHE[key]
    T = 16 + 128 * NT
    cs, ident, rc, p2 = host_constants(T, ITER)
    f = lambda a: np.ascontiguousarray(np.asarray(a, dtype=np.float32))
    shared = {
        "meta": f(meta_tokens), "norm_gain": f(norm_gain), "w_in": f(w_in), "q_norm_gain": f(q_norm_gain),
        "k_norm_gain": f(k_norm_gain), "pool_w": f(pool_w), "pool_b": f(pool_b), "pool_scale": f(pool_scale),
        "w_a": f(w_branch_a), "w_b": f(w_branch_b), "w_out": f(w_out),
        "cs": cs, "ident": ident, "rcnt": rc, "pow2": p2,
    }
    in_maps = []
    for c in range(n_cores):
        m = dict(shared)
        m["x"] = np.ascontiguousarray(x[c * NS:(c + 1) * NS])
        in_maps.append(m)
    res = run_bass_kernel_spmd(nc, in_maps, core_ids=list(range(n_cores)))
    out = np.concatenate([np.asarray(r["out"]) for r in res.results], axis=0)
    return out.astype(np.float32)
```
